# Optimizing a Trainium2 kernel written in Bass

```python
import jax, jax.numpy as jnp
from jax import lax
import numpy as np

D_MODEL = 2048
BATCH = 16
SEQ = 256
DEPTH = 4
DEC_BATCH = 8
DEC_SEQ = 1024
PAST_LEN = 256

GRID_W = 64
N_AB = (DEPTH + 1) // 2
N_POOL = DEPTH // 2
C_A = D_MODEL // 2
HEAD_A = 64
H_A = C_A // HEAD_A
DECAY_LORA = 64
ICLR_LORA = 64
GATE_LORA = 160
C_B = D_MODEL // 2
HEAD_B = 128
H_B = C_B // HEAD_B
CONV_W = 3
CHUNK = 64
C_PA = 3 * C_A + 2 * DECAY_LORA + 2 * ICLR_LORA + GATE_LORA
C_PB = 4 * C_B + 4 * H_B
C_IN = C_PA + C_PB
POOL_WINDOWS = (2, 4, 8, 16)
N_POOL_GROUPS = 4
C_G = D_MODEL // N_POOL_GROUPS
D_FF = ((8 * D_MODEL + 3 * 256 - 1) // (3 * 256)) * 256
RMS_EPS = 1e-6
GN_EPS = 64e-5

kernel_name = 'hybrid_rwkv7_gdn_pool_dit_step'


def _rmsnorm(x, w):
    xf = x.astype(jnp.float32)
    y = xf * lax.rsqrt(jnp.mean(xf * xf, -1, keepdims=True) + RMS_EPS)
    return (y * w.astype(jnp.float32)).astype(x.dtype)


def _l2norm(x, eps):
    return x * lax.rsqrt(jnp.sum(x * x, -1, keepdims=True) + eps)


def _from_prev(x, axis):
    pad = [(0, 0)] * x.ndim
    pad[axis] = (1, 0)
    return lax.slice_in_dim(jnp.pad(x, pad), 0, x.shape[axis], axis=axis)


def _from_next(x, axis):
    pad = [(0, 0)] * x.ndim
    pad[axis] = (0, 1)
    return lax.slice_in_dim(jnp.pad(x, pad), 1, x.shape[axis] + 1, axis=axis)


def _token_shift(p, grid):
    b, t, ch = p.shape
    if grid:
        rows = t // GRID_W
        p4 = p.reshape(b, rows, GRID_W, ch // 4, 4)
        s = jnp.stack([_from_prev(p4[..., 0], 2), _from_next(p4[..., 1], 2),
                       _from_prev(p4[..., 2], 1), _from_next(p4[..., 3], 1)], -1)
    else:
        p2 = p.reshape(b, t, ch // 2, 2)
        s = jnp.stack([_from_prev(p2[..., 0], 1), _from_next(p2[..., 1], 1)], -1)
    return s.reshape(b, t, ch)


def _centred_dwconv(x, w):
    t = x.shape[1]
    half = CONV_W // 2
    xp = jnp.pad(x, ((0, 0), (half, CONV_W - 1 - half), (0, 0)))
    return sum(xp[:, j:j + t] * w[j] for j in range(CONV_W))


def _groupnorm_heads(y, w, b):
    mu = jnp.mean(y, -1, keepdims=True)
    var = jnp.mean(jnp.square(y - mu), -1, keepdims=True)
    return (y - mu) * lax.rsqrt(var + GN_EPS) * w + b


def _rwkv7_scan(r, w, k, v, a, bb, s0, reverse):
    def step(S, inp):
        r_t, w_t, k_t, v_t, a_t, b_t = inp
        sa = jnp.einsum('bhvk,bhk->bhv', S, a_t)
        S = S * w_t[:, :, None, :] + sa[..., None] * b_t[:, :, None, :] + v_t[..., None] * k_t[:, :, None, :]
        return S, jnp.einsum('bhvk,bhk->bhv', S, r_t)
    xs = tuple(jnp.moveaxis(z, 1, 0) for z in (r, w, k, v, a, bb))
    S, ys = lax.scan(step, s0, xs, reverse=reverse)
    return jnp.moveaxis(ys, 0, 1), S


def _gdn_chunked(q, k, v, g, beta, s0):
    bn, t, h, dk = q.shape
    dv = v.shape[-1]
    n = t // CHUNK

    def blk(z):
        return jnp.moveaxis(z.reshape(bn, n, CHUNK, *z.shape[2:]), 3, 2)
    q = blk(q) * (dk ** -0.5)
    k = blk(k)
    v = blk(v)
    beta = blk(beta)
    g = jnp.cumsum(blk(g), axis=-1)
    idx = jnp.arange(CHUNK)
    incl = idx[:, None] >= idx[None, :]
    strict = idx[:, None] > idx[None, :]
    decay = jnp.exp(jnp.where(incl, g[..., :, None] - g[..., None, :], -jnp.inf))
    kb = k * beta[..., None]
    lmat = jnp.einsum('bnhik,bnhjk->bnhij', kb, k) * decay * strict
    amat = lmat + jnp.eye(CHUNK, dtype=lmat.dtype)
    rhs = jnp.concatenate([v * beta[..., None], kb * jnp.exp(g)[..., None]], -1)
    sol = lax.linalg.triangular_solve(amat, rhs, left_side=True, lower=True, unit_diagonal=True)
    u, wy = sol[..., :dv], sol[..., dv:]
    attn = jnp.einsum('bnhik,bnhjk->bnhij', q, k) * decay
    qg = q * jnp.exp(g)[..., None]
    g_last = g[..., -1]
    k_dec = k * jnp.exp(g_last[..., None] - g)[..., None]

    def step(S, inp):
        qg_c, w_c, u_c, attn_c, kd_c, gl_c = inp
        v_new = u_c - jnp.einsum('bhck,bhkv->bhcv', w_c, S)
        o = jnp.einsum('bhck,bhkv->bhcv', qg_c, S) + jnp.einsum('bhij,bhjv->bhiv', attn_c, v_new)
        S = S * jnp.exp(gl_c)[..., None, None] + jnp.einsum('bhck,bhcv->bhkv', kd_c, v_new)
        return S, o
    xs = tuple(jnp.moveaxis(z, 1, 0) for z in (qg, wy, u, attn, k_dec, g_last))
    S, o = lax.scan(step, s0, xs)
    o = jnp.transpose(o, (1, 0, 3, 2, 4)).reshape(bn, t, h, dv)
    return o, S


def _mixer_ab(h, grid, s_rwkv0, s_delta0, w_in, w_out, mu, w0, w2, a0, a2, g2, k_k, k_a, r_k,
              ln_w, ln_b, conv_w, a_log, dt_bias, gn_w):
    f32 = jnp.float32
    b, t, _ = h.shape
    p = h @ w_in
    pa = p[..., :C_PA]
    pa = (pa + (_token_shift(pa, grid) - pa) * mu).astype(f32)
    r, k, v, w_lo, a_lo, g_lo = jnp.split(
        pa, [C_A, 2 * C_A, 3 * C_A, 3 * C_A + 2 * DECAY_LORA, 3 * C_A + 2 * DECAY_LORA + 2 * ICLR_LORA], -1)
    w_lo = jnp.tanh(w_lo.reshape(b, t, 2, DECAY_LORA))
    a_lo = a_lo.reshape(b, t, 2, ICLR_LORA)
    w_log = -jax.nn.softplus(-(w0 + jnp.einsum('btdr,drc->btdc', w_lo, w2))) - 0.5
    decay = jnp.exp(-jnp.exp(w_log))
    iclr = jax.nn.sigmoid(a0 + jnp.einsum('btdr,drc->btdc', a_lo, a2))
    gate = jax.nn.sigmoid(g_lo) @ g2

    def heads(z):
        return z.reshape(b, t, H_A, HEAD_A)
    kk = _l2norm(heads(k * k_k), 1e-12)
    rh, vh = heads(r), heads(v)
    lnw, lnb = ln_w.reshape(H_A, HEAD_A), ln_b.reshape(H_A, HEAD_A)
    ys, fin_a = [], []
    for d in range(2):
        k_d = heads(k * (1 + (iclr[:, :, d] - 1) * k_a))
        y_d, s_d = _rwkv7_scan(rh, heads(decay[:, :, d]), k_d, vh, -kk, kk * heads(iclr[:, :, d]),
                               s_rwkv0[:, d].astype(f32), d == 1)
        ys.append(_groupnorm_heads(y_d, lnw, lnb) + jnp.sum(rh * k_d * r_k, -1, keepdims=True) * vh)
        fin_a.append(s_d)
    o_a = (ys[0] + ys[1]).reshape(b, t, C_A) * gate
    pb = p[..., C_PA:]
    qkv = jax.nn.silu(_centred_dwconv(pb[..., :3 * C_B], conv_w)).astype(f32)
    qd, kd, vd = jnp.split(qkv, 3, -1)
    qd = _l2norm(qd.reshape(b, t, H_B, HEAD_B), 1e-6)
    kd = _l2norm(kd.reshape(b, t, H_B, HEAD_B), 1e-6)
    vd = vd.reshape(b, t, H_B, HEAD_B)
    z = pb[..., 3 * C_B:4 * C_B].astype(f32).reshape(b, t, H_B, HEAD_B)
    beta = jax.nn.sigmoid(pb[..., 4 * C_B:4 * C_B + 2 * H_B].astype(f32)).reshape(b, t, 2, H_B)
    alpha = pb[..., 4 * C_B + 2 * H_B:].astype(f32).reshape(b, t, 2, H_B)
    g_log = -jnp.exp(a_log) * jax.nn.softplus(alpha + dt_bias)
    o_f, s_f = _gdn_chunked(qd, kd, vd, g_log[:, :, 0], beta[:, :, 0], s_delta0[:, 0].astype(f32))
    fl = lambda u_: jnp.flip(u_, 1)
    o_r, s_r = _gdn_chunked(fl(qd), fl(kd), fl(vd), fl(g_log[:, :, 1]), fl(beta[:, :, 1]),
                            s_delta0[:, 1].astype(f32))
    o_bd = o_f + fl(o_r)
    o_bd = o_bd * lax.rsqrt(jnp.mean(o_bd * o_bd, -1, keepdims=True) + RMS_EPS) * gn_w * jax.nn.silu(z)
    o_b = o_bd.reshape(b, t, C_B)
    out = jnp.concatenate([o_a, o_b], -1).astype(h.dtype) @ w_out
    return out, jnp.stack(fin_a, 1).astype(h.dtype), jnp.stack([s_f, s_r], 1).astype(h.dtype)


def _multiscale_pool(h, w_pool, scale):
    f32 = jnp.float32
    b, t, _ = h.shape
    hf = h.astype(f32)
    cs = jnp.pad(jnp.cumsum(hf, 1), ((0, 0), (1, 0), (0, 0)))
    pos = jnp.arange(t)
    means = []
    for gi, win in enumerate(POOL_WINDOWS):
        lo = jnp.clip(pos - win // 2, 0, t)
        hi = jnp.clip(pos - win // 2 + win, 0, t)
        cg = cs[..., gi * C_G:(gi + 1) * C_G]
        s = jnp.take(cg, hi, axis=1) - jnp.take(cg, lo, axis=1)
        means.append(s / (hi - lo).astype(f32)[:, None])
    p = jnp.stack(means, 2) - hf.reshape(b, t, N_POOL_GROUPS, C_G)
    y = jnp.einsum('btgc,gce->btge', p, w_pool).reshape(b, t, D_MODEL)
    return (y * scale).astype(h.dtype)


def _swiglu(h, wg, wu, wd):
    return (jax.nn.silu(h @ wg) * (h @ wu)) @ wd


def _trunk(x, cond, grid, s_rwkv0, s_delta0, shared, ab, pool, keep_state):
    mod_w, mod_b, norm_mix_w, norm_ffn_w, norm_final_w, w_gate, w_up, w_down = shared
    pool_w, pool_scale = pool
    cond_act = jax.nn.silu(cond)
    st_r, st_d = [], []
    for l in range(DEPTH):
        mod = (cond_act @ mod_w[l] + mod_b[l]).reshape(cond.shape[0], 1, 6, D_MODEL)
        shift_m, scale_m, gate_m, shift_f, scale_f, gate_f = (mod[:, :, j] for j in range(6))
        h = _rmsnorm(x, norm_mix_w[l]) * (1 + scale_m) + shift_m
        i = l // 2
        if l % 2 == 0:
            out, s_r, s_d = _mixer_ab(h, grid, s_rwkv0[:, i], s_delta0[:, i], *(prm[i] for prm in ab))
            if keep_state:
                st_r.append(s_r)
                st_d.append(s_d)
        else:
            out = _multiscale_pool(h, pool_w[i], pool_scale[i])
        x = x + gate_m * out
        h = _rmsnorm(x, norm_ffn_w[l]) * (1 + scale_f) + shift_f
        x = x + gate_f * _swiglu(h, w_gate[l], w_up[l], w_down[l])
    y = _rmsnorm(x, norm_final_w)
    if keep_state:
        return y, jnp.stack(st_r, 1), jnp.stack(st_d, 1)
    return y, None, None


def setup_inputs(seed: int = 0) -> dict:
    key = jax.random.key(seed)
    ks = iter(jax.random.split(key, 48))
    f32 = jnp.float32
    D = D_MODEL

    def nrm(shape, s):
        return jax.random.normal(next(ks), shape, f32) * s

    def uni(shape, lo, hi):
        return jax.random.uniform(next(ks), shape, f32, minval=lo, maxval=hi)

    dt = jnp.exp(uni((N_AB, 2, H_B), float(np.log(1e-3)), float(np.log(1e-1))))
    return {
        'x_prompt': nrm((BATCH, SEQ, D), 1.0),
        'x_sample': nrm((DEC_BATCH, DEC_SEQ, D), 1.0),
        'state_rwkv': nrm((DEC_BATCH, N_AB, 2, H_A, HEAD_A, HEAD_A), 1.0),
        'state_delta': nrm((DEC_BATCH, N_AB, 2, H_B, HEAD_B, HEAD_B), 0.3),
        'c': nrm((DEC_BATCH, D), 1.0),
        'c_ctx': nrm((D,), 1.0),
        'mod_w': nrm((DEPTH, D, 6 * D), 0.5 * D ** -0.5),
        'mod_b': nrm((DEPTH, 6 * D), 0.02),
        'norm_mix_w': 1.0 + nrm((DEPTH, D), 0.05),
        'norm_ffn_w': 1.0 + nrm((DEPTH, D), 0.05),
        'norm_final_w': 1.0 + nrm((D,), 0.05),
        'ab_w_in': nrm((N_AB, D, C_IN), D ** -0.5),
        'ab_w_out': nrm((N_AB, C_A + C_B, D), (C_A + C_B) ** -0.5),
        'rwkv_mu': uni((N_AB, C_PA), 0.0, 1.0),
        'rwkv_w0': uni((N_AB, 2, C_A), -6.0, -1.0),
        'rwkv_w2': nrm((N_AB, 2, DECAY_LORA, C_A), 0.1 * DECAY_LORA ** -0.5),
        'rwkv_a0': nrm((N_AB, 2, C_A), 0.1),
        'rwkv_a2': nrm((N_AB, 2, ICLR_LORA, C_A), 0.5 * ICLR_LORA ** -0.5),
        'rwkv_g2': nrm((N_AB, GATE_LORA, C_A), GATE_LORA ** -0.5),
        'rwkv_k_k': 0.85 + nrm((N_AB, C_A), 0.05),
        'rwkv_k_a': 1.0 + nrm((N_AB, C_A), 0.05),
        'rwkv_r_k': nrm((N_AB, H_A, HEAD_A), 0.1),
        'rwkv_ln_w': 1.0 + nrm((N_AB, C_A), 0.05),
        'rwkv_ln_b': nrm((N_AB, C_A), 0.01),
        'gdn_conv_w': nrm((N_AB, CONV_W, 3 * C_B), CONV_W ** -0.5),
        'gdn_a_log': jnp.log(uni((N_AB, 2, H_B), 1.0, 16.0)),
        'gdn_dt_bias': dt + jnp.log(-jnp.expm1(-dt)),
        'gdn_norm_w': 1.0 + nrm((N_AB, HEAD_B), 0.05),
        'pool_w': nrm((N_POOL, N_POOL_GROUPS, C_G, C_G), C_G ** -0.5),
        'pool_scale': 1.0 + nrm((N_POOL, D), 0.1),
        'ffn_w_gate': nrm((DEPTH, D, D_FF), D ** -0.5),
        'ffn_w_up': nrm((DEPTH, D, D_FF), D ** -0.5),
        'ffn_w_down': nrm((DEPTH, D_FF, D), D_FF ** -0.5),
    }


def reference(x_prompt, x_sample, state_rwkv, state_delta, c, c_ctx, mod_w, mod_b, norm_mix_w,
              norm_ffn_w, norm_final_w, ab_w_in, ab_w_out, rwkv_mu, rwkv_w0, rwkv_w2, rwkv_a0, rwkv_a2,
              rwkv_g2, rwkv_k_k, rwkv_k_a, rwkv_r_k, rwkv_ln_w, rwkv_ln_b, gdn_conv_w, gdn_a_log,
              gdn_dt_bias, gdn_norm_w, pool_w, pool_scale, ffn_w_gate, ffn_w_up, ffn_w_down):
    shared = (mod_w, mod_b, norm_mix_w, norm_ffn_w, norm_final_w, ffn_w_gate, ffn_w_up, ffn_w_down)
    ab = (ab_w_in, ab_w_out, rwkv_mu, rwkv_w0, rwkv_w2, rwkv_a0, rwkv_a2, rwkv_g2, rwkv_k_k, rwkv_k_a,
          rwkv_r_k, rwkv_ln_w, rwkv_ln_b, gdn_conv_w, gdn_a_log, gdn_dt_bias, gdn_norm_w)
    pool = (pool_w, pool_scale)
    bp = x_prompt.shape[0]
    zero_r = jnp.zeros((bp, N_AB, 2, H_A, HEAD_A, HEAD_A), x_prompt.dtype)
    zero_d = jnp.zeros((bp, N_AB, 2, H_B, HEAD_B, HEAD_B), x_prompt.dtype)
    y_prompt, new_state_rwkv, new_state_delta = _trunk(x_prompt, c_ctx[None, :], False, zero_r, zero_d,
                                                       shared, ab, pool, True)
    y_sample, _, _ = _trunk(x_sample, c, True, state_rwkv, state_delta, shared, ab, pool, False)
    return (y_prompt, y_sample, new_state_rwkv, new_state_delta)
```

```python
import numpy as np
from contextlib import ExitStack
import concourse.bass as bass
import concourse.mybir as mybir
from concourse.bass_utils import run_bass_kernel_spmd

F32 = mybir.dt.float32
BF16 = mybir.dt.bfloat16
AF = mybir.ActivationFunctionType
ALU = mybir.AluOpType
AX = mybir.AxisListType


class Dep:
    __slots__ = ("w", "r")

    def __init__(self):
        self.w = None
        self.r = {}


class Sched:
    ENG = ("pe", "dve", "act", "pool", "sp")

    def __init__(self, nc, es, n_dma_sems=40, trust=("pe",)):
        self.nc = nc
        self.prog = {k: [] for k in self.ENG}
        self.sems = {}
        for k in self.ENG:
            self.sems[k] = es.enter_context(nc.semaphore("s_" + k))
        self.cnt = {k: 0 for k in self.ENG}
        self.known = {k: {} for k in self.ENG}
        self.trust = set(trust)
        self.ndma = n_dma_sems
        for i in range(n_dma_sems):
            self.sems[("d", i)] = es.enter_context(nc.semaphore("s_d%d" % i))
        self.dcnt = [0] * n_dma_sems
        self.drr = 0
        self.drr_sw = 0
        self.out_events = []

    def _need(self, eng, waits, ev):
        if ev is None:
            return
        k, v = ev
        if k == eng and eng in self.trust:
            return
        if self.known[eng].get(k, 0) >= v:
            return
        if waits.get(k, 0) < v:
            waits[k] = v

    def _collect(self, eng, reads, writes):
        waits = {}
        for d in reads:
            self._need(eng, waits, d.w)
        for d in writes:
            self._need(eng, waits, d.w)
            for k, v in d.r.items():
                self._need(eng, waits, (k, v))
        return waits

    def _commit(self, eng, waits, ev, reads, writes):
        for k, v in waits.items():
            self.known[eng][k] = v
        for d in writes:
            d.w = ev
            d.r = {}
        k, v = ev
        for d in reads:
            if d.r.get(k, 0) < v:
                d.r[k] = v

    def op(self, eng, fn, reads=(), writes=(), force=()):
        waits = self._collect(eng, reads, writes)
        for (fk, fv) in force:
            if self.known[eng].get(fk, 0) < fv and waits.get(fk, 0) < fv:
                waits[fk] = fv
        self.cnt[eng] += 1
        ev = (eng, self.cnt[eng])
        self.prog[eng].append((list(waits.items()), fn, eng, 1))
        self._commit(eng, waits, ev, reads, writes)
        return ev

    def dma(self, eng, fn, reads=(), writes=(), is_output=False):
        waits = self._collect(eng, reads, writes)
        half = self.ndma // 2
        if eng == "pool":
            s = half + self.drr_sw
            self.drr_sw = (self.drr_sw + 1) % (self.ndma - half)
        else:
            s = self.drr
            self.drr = (self.drr + 1) % half
        key = ("d", s)
        if self.dcnt[s] > 0:
            self._need(eng, waits, (key, self.dcnt[s]))
        self.dcnt[s] += 16
        ev = (key, self.dcnt[s])
        self.prog[eng].append((list(waits.items()), fn, key, 16))
        self._commit(eng, waits, ev, reads, writes)
        if is_output:
            self.out_events.append(ev)
        return ev

    def barrier(self):
        snap = {k: self.cnt[k] for k in self.ENG if self.cnt[k] > 0}
        for i in range(self.ndma):
            if self.dcnt[i] > 0:
                snap[("d", i)] = self.dcnt[i]
        for eng in self.ENG:
            waits = {}
            for k, v in snap.items():
                if k == eng and eng in self.trust:
                    continue
                if self.known[eng].get(k, 0) < v:
                    waits[k] = v
            if waits:
                self.prog[eng].append((list(waits.items()), None, None, 0))
                for k, v in waits.items():
                    self.known[eng][k] = v

    def finish(self):
        waits = {}
        for ev in self.out_events:
            self._need("sp", waits, ev)
        self.prog["sp"].append((list(waits.items()), None, None, 0))

    def emit(self, block):
        sems = self.sems

        def mk(engname):
            items = self.prog[engname]

            def body(e):
                for waits, fn, inck, incv in items:
                    for k, v in waits:
                        e.wait_ge(sems[k], v)
                    if fn is not None:
                        fn(e).then_inc(sems[inck], incv)
            return body
        block.tensor(mk("pe"))
        block.vector(mk("dve"))
        block.scalar(mk("act"))
        block.gpsimd(mk("pool"))
        block.sync(mk("sp"))


class T:
    __slots__ = ("ap", "deps", "ps")

    def __init__(self, ap, deps, ps=False):
        self.ap = ap
        self.deps = deps
        self.ps = ps

    def __getitem__(self, k):
        return T(self.ap[k], self.deps, self.ps)

    def v(self, fn):
        return T(fn(self.ap), self.deps, self.ps)

    def re(self, s, **kw):
        return T(self.ap.rearrange(s, **kw), self.deps, self.ps)

    def bc(self, shape):
        return T(self.ap.to_broadcast(list(shape)), self.deps, self.ps)


class Arena:
    def __init__(self, nc, es, kbytes, parent=None, base=0, words=0):
        if parent is None:
            self.words = kbytes * 256
            self.t = es.enter_context(nc.sbuf_tensor("arena", [128, self.words], F32))
            self.base = 0
        else:
            self.words = words
            self.t = parent.t
            self.base = base
        self.off = 0
        self.peak = 0
        self.live = []
        self.ghosts = []

    def alloc(self, shape, dtype=F32, deps=None):
        n = int(np.prod(shape))
        w = n if dtype == F32 else (n + 1) // 2
        assert self.off + w <= self.words, "arena overflow %d + %d > %d" % (self.off, w, self.words)
        s0, e0 = self.off, self.off + w
        v = self.t[:, self.base + s0:self.base + e0]
        self.off += w
        self.peak = max(self.peak, self.off)
        if dtype != F32:
            v = v.bitcast(dtype)
            if 2 * w != n:
                v = v[:, 0:n]
        if len(shape) == 2:
            v = v.rearrange("p (a b) -> p a b", a=shape[0])
        elif len(shape) == 3:
            v = v.rearrange("p (a b c) -> p a b c", a=shape[0], b=shape[1])
        deps = [Dep()] if deps is None else deps
        keep = []
        for (gs, ge, gd) in self.ghosts:
            if gs < e0 and s0 < ge:
                for od in gd:
                    evs = list(od.r.items())
                    if od.w is not None:
                        evs.append(od.w)
                    for nd in deps:
                        for kk_, vv_ in evs:
                            if nd.r.get(kk_, 0) < vv_:
                                nd.r[kk_] = vv_
                if gs >= s0 and ge <= e0:
                    continue
            keep.append((gs, ge, gd))
        self.ghosts = keep
        self.live.append((s0, e0, deps))
        return T(v, deps)

    def mark(self):
        return self.off

    def release(self, m):
        keep = []
        for (s0, e0, dd) in self.live:
            if e0 > m:
                self.ghosts.append((max(s0, m), e0, dd))
                if s0 < m:
                    keep.append((s0, m, dd))
            else:
                keep.append((s0, e0, dd))
        self.live = keep
        self.off = m


def _sc(x):
    return x.ap if isinstance(x, T) else x


def _dp(*xs):
    out = []
    for x in xs:
        if isinstance(x, T):
            out.extend(x.deps)
    return out


def _wr(out, *ins):
    w = list(out.deps)
    for x in ins:
        if isinstance(x, T) and x.ps:
            w.extend(x.deps)
    return w


class K:
    def __init__(self, nc, es, arena_kb):
        self.nc = nc
        self.S = Sched(nc, es)
        self.A = Arena(nc, es, arena_kb)
        self.psum = es.enter_context(nc.psum_tensor("ps", [128, 8, 512], F32))
        self.pdeps = [Dep() for _ in range(8)]
        self.pi = 0
        self.rg = {}
        self.reserved = -1

    def bank(self):
        i = self.pi % 8
        self.pi += 1
        if i == self.reserved:
            i = self.pi % 8
            self.pi += 1
        return T(self.psum[:, i, :], [self.pdeps[i]], True)

    def tt(self, out, a, b, op, eng="dve"):
        return self.S.op(eng, lambda e: e.tensor_tensor(out=out.ap, in0=a.ap, in1=b.ap, op=op), reads=_dp(a, b), writes=_wr(out, a, b))

    def ts(self, out, a, s1, op0, s2=None, op1=None, eng="dve"):
        if op1 is None:
            return self.S.op(eng, lambda e: e.tensor_scalar(out=out.ap, in0=a.ap, scalar1=_sc(s1), scalar2=None, op0=op0), reads=_dp(a, s1), writes=_wr(out, a))
        return self.S.op(eng, lambda e: e.tensor_scalar(out=out.ap, in0=a.ap, scalar1=_sc(s1), scalar2=_sc(s2), op0=op0, op1=op1), reads=_dp(a, s1, s2), writes=_wr(out, a))

    def stt(self, out, a, s, b, op0, op1):
        return self.S.op("dve", lambda e: e.scalar_tensor_tensor(out=out.ap, in0=a.ap, scalar=_sc(s), in1=b.ap, op0=op0, op1=op1), reads=_dp(a, s, b), writes=_wr(out, a, b))

    def act(self, out, a, func, scale=1.0, bias=0.0):
        return self.S.op("act", lambda e: e.activation(out=out.ap, in_=a.ap, func=func, scale=_sc(scale), bias=_sc(bias)), reads=_dp(a, scale, bias), writes=_wr(out, a))

    def act_acc(self, out, a, func, acc):
        return self.S.op("act", lambda e: e.activation(out=out.ap, in_=a.ap, func=func, accum_out=acc.ap), reads=_dp(a), writes=_wr(out, a) + acc.deps)

    def copy(self, out, a, eng="dve"):
        if eng == "act":
            return self.S.op("act", lambda e: e.copy(out=out.ap, in_=a.ap), reads=_dp(a), writes=_wr(out, a))
        return self.S.op(eng, lambda e: e.tensor_copy(out=out.ap, in_=a.ap), reads=_dp(a), writes=_wr(out, a))

    def memset(self, out, val, eng="dve"):
        return self.S.op(eng, lambda e: e.memset(out.ap, val), writes=out.deps)

    def recip(self, out, a):
        return self.S.op("dve", lambda e: e.reciprocal(out=out.ap, in_=a.ap), reads=_dp(a), writes=_wr(out, a))

    def scan(self, out, d0, d1, init, op0, op1):
        return self.S.op("dve", lambda e: e.tensor_tensor_scan(out=out.ap, data0=d0.ap, data1=d1.ap, initial=init, op0=op0, op1=op1), reads=_dp(d0, d1), writes=_wr(out, d0, d1))

    def _rowgrp_force(self, out, lhsT):
        rg = (lhsT.ap.base_partition(), lhsT.ap.shape[0])
        force = []
        for d in out.deps:
            prev = self.rg.get(id(d))
            if prev is not None and prev[0] != rg:
                force.append(prev[1])
        return rg, force

    def mm(self, out, lhsT, rhs, start=True, stop=True):
        rg, force = self._rowgrp_force(out, lhsT)
        ev = self.S.op("pe", lambda e: e.matmul(out.ap, lhsT=lhsT.ap, rhs=rhs.ap, start=start, stop=stop), reads=_dp(lhsT, rhs), writes=out.deps, force=force)
        for d in out.deps:
            self.rg[id(d)] = (rg, ev)
        return ev

    def tr(self, out, a, ident):
        rg, force = self._rowgrp_force(out, a)
        ev = self.S.op("pe", lambda e: e.transpose(out=out.ap, in_=a.ap, identity=ident.ap), reads=_dp(a, ident), writes=out.deps, force=force)
        for d in out.deps:
            self.rg[id(d)] = (rg, ev)
        return ev

    def bnstats(self, out, a):
        return self.S.op("dve", lambda e: e.bn_stats(out=out.ap, in_=a.ap), reads=_dp(a), writes=_wr(out, a))

    def bnaggr(self, out, a):
        return self.S.op("dve", lambda e: e.bn_aggr(out=out.ap, in_=a.ap), reads=_dp(a), writes=_wr(out, a))

    def dma(self, out, a, eng="sp", is_output=False):
        return self.S.dma(eng, lambda e: e.dma_start(out=out.ap, in_=a.ap), reads=_dp(a), writes=out.deps, is_output=is_output)

    def rsqrt(self, out, a, eps, scale=1.0):
        self.act(out, a, AF.Sqrt, scale=scale, bias=eps)
        self.recip(out, out)


class Cfg:
    def __init__(self, D=2048, DEPTH=4, T_S=1024, T_P=256, DFF=None):
        self.D = D
        self.DC = D // 128
        self.DEPTH = DEPTH
        self.N_AB = (DEPTH + 1) // 2
        self.N_POOL = DEPTH // 2
        self.T_S = T_S
        self.T_P = T_P
        self.NT = T_S + 2 * T_P
        self.seqs = [(0, T_S, True, 1), (T_S, T_P, False, 0), (T_S + T_P, T_P, False, 0)]
        self.DFF = DFF if DFF is not None else ((8 * D + 3 * 256 - 1) // (3 * 256)) * 256
        self.FCN = self.DFF // 128
        self.C_A = D // 2
        self.H_A = self.C_A // 64
        self.NHP = self.C_A // 128
        self.C_B = D // 2
        self.H_B = self.C_B // 128
        self.C_PA = 3 * self.C_A + 416
        self.C_PB = 4 * self.C_B + 4 * self.H_B
        self.C_IN = self.C_PA + self.C_PB
        self.NMU = (self.C_PA + 127) // 128
        self.C_G = D // 4
        self.GRID_W = 64
        self.tblocks = []
        for (t0, tl, g, c) in [(0, T_S, True, 1), (T_S, 2 * T_P, False, 0)]:
            o = 0
            while o < tl:
                n = min(512, tl - o)
                self.tblocks.append((t0 + o, n, c))
                o += n


def host_consts(cfg):
    c = {}
    c["ident"] = np.eye(128, dtype=np.float32)
    p = np.arange(128)
    pm = np.zeros((128, 8), np.float32)
    pm[:, 0] = (p % 2 == 0)
    pm[:, 1] = (p % 2 == 1)
    for j in range(4):
        pm[:, 2 + j] = (p % 4 == j)
    c["pmask"] = pm
    j = np.arange(64)[:, None]
    i = np.arange(64)[None, :]
    tm = np.zeros((64, 8, 64), np.float32)
    tm[:, 0] = (i > j)
    tm[:, 1] = (i >= j)
    tm[:, 2] = (i < j)
    tm[:, 3] = (i <= j)
    tm[:, 4] = (i == j)
    c["tmask"] = np.concatenate([tm, np.zeros((64, 8, 64), np.float32)], 0)
    bo = np.zeros((128, 128), np.float32)
    bo[:64, :64] = 1
    bo[64:, 64:] = 1
    c["blockones"] = bo
    for tl in sorted({cfg.T_S, cfg.T_P}):
        pos = np.arange(tl)
        ic = np.zeros((4, tl), np.float32)
        for gi, win in enumerate((2, 4, 8, 16)):
            lo = np.clip(pos - win // 2, 0, tl)
            hi = np.clip(pos - win // 2 + win, 0, tl)
            ic[gi] = 1.0 / (hi - lo)
        c["invcnt%d" % tl] = np.broadcast_to(ic[None], (128, 4, tl)).copy()
    return c


def vec_layout(cfg):
    CAC = cfg.C_A // 128
    CBC = cfg.C_B // 128
    items = [("cond", cfg.DC * 2), ("modb", cfg.DEPTH * 6 * cfg.DC), ("nmix", cfg.DEPTH * cfg.DC),
             ("nffn", cfg.DEPTH * cfg.DC), ("nfin", cfg.DC), ("mu", cfg.N_AB * cfg.NMU),
             ("w0", cfg.N_AB * 2 * CAC), ("a0", cfg.N_AB * 2 * CAC), ("kk", cfg.N_AB * CAC), ("ka", cfg.N_AB * CAC),
             ("lnw", cfg.N_AB * CAC), ("lnb", cfg.N_AB * CAC), ("rk", cfg.N_AB * CAC),
             ("convw", cfg.N_AB * 3 * 3 * CBC), ("gnw", cfg.N_AB), ("alog", cfg.N_AB), ("dtb", cfg.N_AB),
             ("pscale", max(1, cfg.N_POOL) * cfg.DC), ("pmask", 8)]
    lay = {}
    o = 0
    for n, sz in items:
        lay[n] = (o, sz)
        o += sz
    return lay, o


def chunked(v):
    v = np.asarray(v, np.float32)
    lead = v.shape[:-1]
    n = v.shape[-1] // 128
    v = v.reshape(lead + (n, 128))
    v = np.moveaxis(v, -1, 0)
    return np.ascontiguousarray(v).reshape(128, -1)


def prep_vecs(cfg, inp, core, consts):
    lay, nv = vec_layout(cfg)
    V = np.zeros((128, nv), np.float32)

    def put(name, arr):
        o, sz = lay[name]
        assert arr.shape == (128, sz), (name, arr.shape, sz)
        V[:, o:o + sz] = arr
    cond = np.stack([chunked(inp["c_ctx"]), chunked(inp["c"][core])], -1).reshape(128, -1)
    put("cond", cond)
    put("modb", chunked(inp["mod_b"].reshape(cfg.DEPTH, 6, cfg.D)))
    put("nmix", chunked(inp["norm_mix_w"]))
    put("nffn", chunked(inp["norm_ffn_w"]))
    put("nfin", chunked(inp["norm_final_w"]))
    mu = np.zeros((cfg.N_AB, cfg.NMU * 128), np.float32)
    mu[:, :cfg.C_PA] = inp["rwkv_mu"]
    put("mu", chunked(mu))
    put("w0", chunked(inp["rwkv_w0"]))
    put("a0", chunked(inp["rwkv_a0"]))
    put("kk", chunked(inp["rwkv_k_k"]))
    put("ka", chunked(inp["rwkv_k_a"]))
    put("lnw", chunked(inp["rwkv_ln_w"]))
    put("lnb", chunked(inp["rwkv_ln_b"]))
    put("rk", chunked(inp["rwkv_r_k"].reshape(cfg.N_AB, cfg.C_A)))
    put("convw", chunked(inp["gdn_conv_w"]))
    put("gnw", np.ascontiguousarray(np.asarray(inp["gdn_norm_w"], np.float32).T))
    al = np.zeros((128, cfg.N_AB), np.float32)
    al[:2 * cfg.H_B] = np.asarray(inp["gdn_a_log"], np.float32).reshape(cfg.N_AB, -1).T
    put("alog", al)
    db = np.zeros((128, cfg.N_AB), np.float32)
    db[:2 * cfg.H_B] = np.asarray(inp["gdn_dt_bias"], np.float32).reshape(cfg.N_AB, -1).T
    put("dtb", db)
    if cfg.N_POOL > 0:
        put("pscale", chunked(inp["pool_scale"]))
    put("pmask", consts["pmask"])
    return V


class MKB:
    def __init__(self, cfg, skip_ab=False, arena_kb=206):
        self.cfg = cfg
        self.skip_ab = skip_ab
        self.arena_kb = arena_kb

    def declare(self, nc):
        c = self.cfg
        d = {}

        def inp(name, shape, dt=F32):
            d[name] = T(nc.dram_tensor(name, list(shape), dt, kind="ExternalInput").ap(), [])

        def outp(name, shape):
            d[name] = T(nc.dram_tensor(name, list(shape), F32, kind="ExternalOutput").ap(), [Dep()])
        lay, nv = vec_layout(c)
        self.lay = lay
        inp("xs", [c.NT, c.D])
        inp("vecs", [128, nv])
        inp("st_r_in", [c.N_AB * 2 * c.H_A * 64, 64])
        inp("st_d_in", [c.N_AB * 2 * c.H_B * 128, 128])
        inp("mod_w", [c.DEPTH, c.D, 6 * c.D])
        inp("w_in", [c.N_AB, c.D, c.C_IN])
        inp("w_out", [c.N_AB, c.D, c.D])
        inp("w2", [c.N_AB, 128, c.C_A])
        inp("a2", [c.N_AB, 128, c.C_A])
        inp("g2", [c.N_AB, 160, c.C_A])
        inp("pool_w", [max(1, c.N_POOL), 4, c.C_G, c.C_G])
        inp("wg", [c.DEPTH, c.D, c.DFF])
        inp("wu", [c.DEPTH, c.D, c.DFF])
        inp("wd", [c.DEPTH, c.DFF, c.D])
        inp("ident", [128, 128])
        inp("tmask", [128, 8, 64])
        inp("blockones", [128, 128])
        for tl in sorted({c.T_S, c.T_P}):
            inp("invcnt%d" % tl, [128, 4, tl])
        outp("y", [c.NT, c.D])
        outp("st_r", [2 * c.N_AB * 2 * c.H_A * 64, 64])
        outp("st_d", [2 * c.N_AB * 2 * c.H_B * 128, 128])
        if not self.skip_ab:
            d["xspill"] = T(nc.dram_tensor("xspill", [128, c.DC * c.NT], F32, kind="Internal").ap(), [Dep()])
            d["rowsc"] = T(nc.dram_tensor("rowsc", [96, c.NT], F32, kind="Internal").ap(), [Dep()])
            d["oscr"] = T(nc.dram_tensor("oscr", [128, c.DC, c.NT], BF16, kind="Internal").ap(), [])
        self.d = d

    def vec(self, name, idx, n=1):
        o, sz = self.lay[name]
        assert idx + n <= sz
        return self.vecs[:, o + idx:o + idx + n]

    def build(self):
        c = self.cfg
        nc = bass.Bass("TRN2", target_bir_lowering=False)
        self.nc = nc
        self.declare(nc)
        with ExitStack() as es:
            k = K(nc, es, self.arena_kb)
            self.k = k
            A = k.A
            d = self.d
            lay, nv = vec_layout(c)
            self.xT_base = A.off
            self.xT = A.alloc([c.DC, c.NT], deps=[])
            self.xdep = [[Dep() for _ in c.tblocks] for _ in range(c.DC)]
            self.vecs = A.alloc([nv])
            self.ident = A.alloc([128])
            self.identb = A.alloc([128], BF16)
            self.onesb = A.alloc([128], BF16)
            self.onesf = A.alloc([128])
            self.blockones = A.alloc([128])
            self.tmask = A.alloc([8, 64])
            self.modT_all = [A.alloc([6 * c.DC, 2]) for _ in range(c.DEPTH)]
            self.modT = self.modT_all[0]
            self.dvec = A.alloc([16 + 4 * c.DC * 2])
            k.dma(self.vecs, d["vecs"])
            k.dma(self.ident, d["ident"])
            k.dma(self.blockones, d["blockones"])
            k.dma(self.tmask, d["tmask"])
            k.copy(self.identb, self.ident)
            k.memset(self.onesb, 1.0)
            k.memset(self.onesf, 1.0)
            self.condact = A.alloc([c.DC, 2], BF16)
            k.act(self.condact.re("p a b -> p (a b)"), self.vec("cond", 0, c.DC * 2), AF.Silu)
            self.marks = []
            self.mark_phase = lambda nm: self.marks.append((nm, k.S.cnt["pe"]))
            import os
            dbg = os.environ.get("MK_DBG", "")
            mm_ = A.mark()
            self.modwb = [A.alloc([c.DC, 512], BF16) for _ in range(2)]
            self.modbi = 0
            self.compute_mod(0)
            A.release(mm_)
            self.load_x()
            self.mod_done = {0}
            self.bg = None
            for l in range(c.DEPTH):
                self.mark_phase("L%d mod" % l)
                if l not in self.mod_done:
                    mm_ = A.mark()
                    self.modwb = [A.alloc([c.DC, 512], BF16) for _ in range(2)]
                    self.compute_mod(l)
                    A.release(mm_)
                    self.mod_done.add(l)
                self.derive_mod(l)
                self.mark_phase("L%d mixer" % l)
                if l % 2 == 0:
                    if not self.skip_ab:
                        self.ab_layer(l)
                elif "nopool" not in dbg:
                    self.pool_layer(l)
                self.mark_phase("L%d ffn" % l)
                if "noffn" not in dbg:
                    self.ffn_layer(l)
            self.mark_phase("final")
            self.final_store()
            self.mark_phase("end")
            k.S.finish()
            self.stats = dict(arena_peak_kb=A.peak / 256, instrs={e: len(v) for e, v in k.S.prog.items()})
            with nc.Block() as block:
                k.S.emit(block)
        return nc

    def xt(self, cidx, bi):
        t0, n, _ = self.cfg.tblocks[bi]
        return T(self.xT.ap[:, cidx, t0:t0 + n], [self.xdep[cidx][bi]])

    def xt_deps(self, t0, n):
        out = []
        for bi, (b0, bn, _) in enumerate(self.cfg.tblocks):
            if b0 < t0 + n and t0 < b0 + bn:
                out.append(bi)
        return out

    def load_x(self):
        c, k, A = self.cfg, self.k, self.k.A
        m = A.mark()
        stg = [A.alloc([c.D]) for _ in range(2)]
        for tt in range(c.NT // 128):
            s = stg[tt % 2]
            k.dma(s, self.d["xs"][tt * 128:(tt + 1) * 128, :])
            bi = [i for i, (b0, bn, _) in enumerate(c.tblocks) if b0 <= tt * 128 < b0 + bn][0]
            for g in range(c.DC // 4):
                pb = k.bank()
                for j in range(4):
                    cc = g * 4 + j
                    k.tr(pb[:, j * 128:(j + 1) * 128], s[:, cc * 128:(cc + 1) * 128], self.ident)
                dst = T(self.xT.ap[:, g * 4:(g + 1) * 4, tt * 128:(tt + 1) * 128], [self.xdep[cc][bi] for cc in range(g * 4, g * 4 + 4)])
                k.copy(dst, pb.re("p (a b) -> p a b", a=4), eng=("dve" if g % 2 == 0 else "act"))
        A.release(m)
        k.S.barrier()

    def compute_mod(self, l):
        c, k, A = self.cfg, self.k, self.k.A
        CB = 512
        nblk = (6 * c.D) // CB
        src = self.d["mod_w"].ap[l].rearrange("(kc p) n -> p kc n", p=128)
        pb = k.bank()
        for b in range(nblk):
            w = self.modwb[self.modbi % 2]
            self.modbi += 1
            k.dma(w, T(src[:, :, b * CB:(b + 1) * CB], []), eng="pool")
            for j in range(CB // 128):
                oc = b * (CB // 128) + j
                for kc in range(c.DC):
                    k.mm(pb[:, oc * 2:oc * 2 + 2], w[:, kc, j * 128:(j + 1) * 128], self.condact[:, kc, :], start=(kc == 0), stop=(kc == c.DC - 1))
        n6 = 6 * c.DC
        ob, _ = self.lay["modb"]
        mb = self.vecs[:, ob + l * n6: ob + (l + 1) * n6]
        k.tt(self.modT_all[l], pb[:, 0:n6 * 2].re("p (a b) -> p a b", b=2), mb.v(lambda a: a.unsqueeze(2)).bc([128, n6, 2]), ALU.add)

    def mod_bg_gen(self, l, buf):
        c, k = self.cfg, self.k
        src = self.d["mod_w"].ap[l].rearrange("(kc p) n -> p kc n", p=128)
        ri = 7
        k.reserved = ri
        pb = T(k.psum[:, ri, :], [k.pdeps[ri]], True)
        n6 = 6 * c.DC
        for oc in range(n6):
            for hh in range(2):
                c0 = oc * 128 + hh * 64
                k.dma(buf, T(src[:, :, c0:c0 + 64], []), eng="pool")
                for kc in range(c.DC):
                    k.mm(pb[64 * hh:64 * hh + 64, oc * 2:oc * 2 + 2], buf[:, kc, :], self.condact[:, kc, :], start=(kc == 0), stop=(kc == c.DC - 1))
                yield
        ob, _ = self.lay["modb"]
        mb = self.vecs[:, ob + l * n6: ob + (l + 1) * n6]
        k.tt(self.modT_all[l], pb[:, 0:n6 * 2].re("p (a b) -> p a b", b=2), mb.v(lambda a: a.unsqueeze(2)).bc([128, n6, 2]), ALU.add)
        k.reserved = -1
        self.mod_done.add(l)

    def bg_step(self, n=1):
        for _ in range(n):
            if self.bg is not None:
                try:
                    next(self.bg)
                except StopIteration:
                    self.bg = None

    def derive_mod(self, l):
        c, k = self.cfg, self.k
        DC = c.DC
        self.modT = self.modT_all[l]
        self.AM = self.dvec[:, 16:16 + DC * 2].re("p (a b) -> p a b", b=2)
        self.AFv = self.dvec[:, 16 + DC * 2:16 + DC * 4].re("p (a b) -> p a b", b=2)
        on, _ = self.lay["nmix"]
        of, _ = self.lay["nffn"]
        nm = self.vecs[:, on + l * DC:on + (l + 1) * DC].v(lambda a: a.unsqueeze(2)).bc([128, DC, 2])
        nf = self.vecs[:, of + l * DC:of + (l + 1) * DC].v(lambda a: a.unsqueeze(2)).bc([128, DC, 2])
        k.stt(self.AM, self.modT[:, 1 * DC:2 * DC, :], 1.0, nm, ALU.add, ALU.mult)
        k.stt(self.AFv, self.modT[:, 4 * DC:5 * DC, :], 1.0, nf, ALU.add, ALU.mult)

    def mod(self, j, cidx, cond):
        return self.modT[:, j * self.cfg.DC + cidx, cond:cond + 1]

    def block_rstd(self, bi, out):
        c, k, A = self.cfg, self.k, self.k.A
        t0, n, _ = c.tblocks[bi]
        m = A.mark()
        sq = [A.alloc([n], BF16) for _ in range(2)]
        pb = k.bank()
        for cc in range(c.DC):
            k.act(sq[cc % 2], self.xt(cc, bi), AF.Square)
            k.mm(pb[:, 0:n], self.onesb, sq[cc % 2], start=(cc == 0), stop=(cc == c.DC - 1))
        k.rsqrt(out, pb[:, 0:n], 1e-6, scale=1.0 / c.D)
        A.release(m)

    def norm_block(self, bi, Avec, shift_j, dst_fn, rs=None):
        c, k, A = self.cfg, self.k, self.k.A
        t0, n, cond = c.tblocks[bi]
        m = A.mark()
        if rs is None:
            rs = A.alloc([n])
            self.block_rstd(bi, rs)
        tmp = [A.alloc([n]) for _ in range(2)]
        for cc in range(c.DC):
            k.tt(tmp[cc % 2], self.xt(cc, bi), rs, ALU.mult)
            k.act(dst_fn(cc), tmp[cc % 2], AF.Identity, scale=Avec[:, cc, cond:cond + 1], bias=self.mod(shift_j, cc, cond))
        A.release(m)

    def ffn_layer(self, l):
        c, k, A = self.cfg, self.k, self.k.A
        m = A.mark()
        hdep = [Dep() for _ in c.tblocks]
        hT = A.alloc([c.DC, c.NT], BF16, deps=hdep)
        for bi, (t0, n, cond) in enumerate(c.tblocks):
            self.norm_block(bi, self.AFv, 3, lambda cc: T(hT.ap[:, cc, t0:t0 + n], [hdep[bi]]))
        import os
        dbg = os.environ.get("MK_DBG", "")
        if "h2x" in dbg:
            for bi, (t0, n, cond) in enumerate(c.tblocks):
                for cc in range(c.DC):
                    k.copy(self.xt(cc, bi), T(hT.ap[:, cc, t0:t0 + n], [hdep[bi]]))
            A.release(m)
            k.S.barrier()
            return
        FB = 2 if c.FCN % 2 == 0 else 1
        NW = 2
        NWD = 1
        wgb = [A.alloc([c.DC, FB * 128], BF16) for _ in range(NW)]
        wub = [A.alloc([c.DC, FB * 128], BF16) for _ in range(NW)]
        wdb = [A.alloc([FB, c.D], BF16) for _ in range(NWD)]
        adep = [[Dep() for _ in c.tblocks] for _ in range(FB)]
        actT = A.alloc([FB, c.NT], BF16, deps=[x_ for r_ in adep for x_ in r_])
        sg = [A.alloc([512]) for _ in range(2)]
        wg_v = self.d["wg"].ap[l].rearrange("(kc p) f -> p kc f", p=128)
        wu_v = self.d["wu"].ap[l].rearrange("(kc p) f -> p kc f", p=128)
        wd_v = self.d["wd"].ap[l].rearrange("(fc p) d -> p fc d", p=128)
        si = 0
        for fb in range(c.FCN // FB):
            w = fb % NW
            fs = slice(fb * FB * 128, (fb + 1) * FB * 128)
            k.dma(wgb[w], T(wg_v[:, :, fs], []), eng="pool")
            k.dma(wub[w], T(wu_v[:, :, fs], []), eng="pool")
            wd_ = wdb[fb % NWD]
            k.dma(wd_, T(wd_v[:, fb * FB:(fb + 1) * FB, :], []), eng="pool")
            if "w2x" in dbg:
                for bi, (t0, n, cond) in enumerate(c.tblocks):
                    for cc in range(c.DC):
                        k.copy(self.xt(cc, bi), wgb[w][:, cc, 0:n])
                break
            for fc in range(FB):
                for bi, (t0, n, cond) in enumerate(c.tblocks):
                    pg = k.bank()
                    pu = k.bank()
                    h = lambda kc: T(hT.ap[:, kc, t0:t0 + n], [hdep[bi]])
                    for kc in range(c.DC):
                        k.mm(pg[:, 0:n], wgb[w][:, kc, fc * 128:(fc + 1) * 128], h(kc), start=(kc == 0), stop=(kc == c.DC - 1))
                    for kc in range(c.DC):
                        k.mm(pu[:, 0:n], wub[w][:, kc, fc * 128:(fc + 1) * 128], h(kc), start=(kc == 0), stop=(kc == c.DC - 1))
                    s = sg[si % 2]
                    si += 1
                    k.act(s[:, 0:n], pg[:, 0:n], AF.Silu)
                    k.tt(T(actT.ap[:, fc, t0:t0 + n], [adep[fc][bi]]), s[:, 0:n], pu[:, 0:n], ALU.mult)
                    if "g2x" in dbg and fb == 0 and fc < c.DC:
                        k.copy(self.xt(fc, bi), pg[:, 0:n])
                    if "a2x" in dbg and fb == 0 and fc < c.DC:
                        k.copy(self.xt(fc, bi), T(actT.ap[:, fc, t0:t0 + n], [adep[fc][bi]]))
            if "g2x" in dbg or "a2x" in dbg:
                break
            for dc in range(c.DC):
                for bi, (t0, n, cond) in enumerate(c.tblocks):
                    po = k.bank()
                    for fc in range(FB):
                        k.mm(po[:, 0:n], wd_[:, fc, dc * 128:(dc + 1) * 128], T(actT.ap[:, fc, t0:t0 + n], [adep[fc][bi]]), start=(fc == 0), stop=(fc == FB - 1))
                    x = self.xt(dc, bi)
                    k.stt(x, po[:, 0:n], self.mod(5, dc, cond), x, ALU.mult, ALU.add)
        A.release(m)
        k.S.barrier()

    def pool_layer(self, l):
        c, k, A = self.cfg, self.k, self.k.A
        i = l // 2
        m = A.mark()
        GC = c.DC // 4
        rs = [A.alloc([n]) for (t0, n, cond) in c.tblocks]
        for bi in range(len(c.tblocks)):
            self.block_rstd(bi, rs[bi])
        gs = A.alloc([c.DC, 2])
        op_, _ = self.lay["pscale"]
        ps = self.vecs[:, op_ + i * c.DC:op_ + (i + 1) * c.DC].v(lambda a: a.unsqueeze(2)).bc([128, c.DC, 2])
        k.tt(gs, self.modT[:, 2 * c.DC:3 * c.DC, :], ps, ALU.mult)
        PAD = 8
        invc = {}
        for tl in sorted({c.T_S, c.T_P}):
            invc[tl] = A.alloc([4, tl])
            k.dma(invc[tl], self.d["invcnt%d" % tl])
        wp = [A.alloc([GC, c.C_G], BF16) for _ in range(2)]
        for gi in range(4):
            win = (2, 4, 8, 16)[gi]
            w = wp[gi % 2]
            k.dma(w, T(self.d["pool_w"].ap[i, gi].rearrange("(kc p) e -> p kc e", p=128), []), eng="pool")
            pT = A.alloc([GC, c.NT], BF16) if gi == 0 else pT
            for g in range(GC):
                cc = gi * GC + g
                hp = A.alloc([c.NT + 2 * PAD * len(c.seqs)]) if (gi == 0 and g == 0) else hp
                wa = A.alloc([c.NT + 2 * PAD * len(c.seqs)]) if (gi == 0 and g == 0) else wa
                wb_ = A.alloc([c.NT + 2 * PAD * len(c.seqs)]) if (gi == 0 and g == 0) else wb_
                k.memset(hp, 0.0, eng="pool")
                for bi, (t0, n, cond) in enumerate(c.tblocks):
                    for si_, (s0, sl, grid, scond) in enumerate(c.seqs):
                        a0 = max(t0, s0)
                        a1 = min(t0 + n, s0 + sl)
                        if a0 >= a1:
                            continue
                        off = PAD * (2 * si_ + 1)
                        tmp = A.alloc([a1 - a0])
                        k.tt(tmp, T(self.xT.ap[:, cc, a0:a1], [self.xdep[cc][bi]]), rs[bi][:, a0 - t0:a1 - t0], ALU.mult)
                        k.act(hp[:, off + a0:off + a1], tmp, AF.Identity, scale=self.AM[:, cc, cond:cond + 1], bias=self.mod(0, cc, cond))
                        A.release(A.mark() - (a1 - a0))
                for si_, (s0, sl, grid, scond) in enumerate(c.seqs):
                    off = PAD * (2 * si_ + 1)
                    lo = off + s0 - PAD
                    L = sl + 2 * PAD
                    H = hp[:, lo:lo + L]
                    W1 = wa[:, lo:lo + L]
                    W2 = wb_[:, lo:lo + L]
                    k.tt(W1[:, 1:L], H[:, 0:L - 1], H[:, 1:L], ALU.add)
                    cur = W1
                    nxt = W2
                    if win >= 4:
                        k.tt(nxt[:, 2:L - 1], cur[:, 1:L - 2], cur[:, 3:L], ALU.add)
                        cur, nxt = nxt, cur
                    if win >= 8:
                        k.tt(nxt[:, 4:L - 3], cur[:, 2:L - 5], cur[:, 6:L - 1], ALU.add)
                        cur, nxt = nxt, cur
                    if win >= 16:
                        k.tt(nxt[:, 8:L - 7], cur[:, 4:L - 11], cur[:, 12:L - 3], ALU.add)
                        cur, nxt = nxt, cur
                    k.tt(nxt[:, PAD:PAD + sl], cur[:, PAD:PAD + sl], invc[sl][:, gi, :], ALU.mult)
                    k.tt(pT[:, g, s0:s0 + sl], nxt[:, PAD:PAD + sl], H[:, PAD:PAD + sl], ALU.subtract)
            for e in range(GC):
                ec = gi * GC + e
                for bi, (t0, n, cond) in enumerate(c.tblocks):
                    po = k.bank()
                    for g in range(GC):
                        k.mm(po[:, 0:n], w[:, g, e * 128:(e + 1) * 128], pT[:, g, t0:t0 + n], start=(g == 0), stop=(g == GC - 1))
                    x = self.xt(ec, bi)
                    k.stt(x, po[:, 0:n], gs[:, ec, cond:cond + 1], x, ALU.mult, ALU.add)
        A.release(m)
        k.S.barrier()

    def final_store(self):
        c, k, A = self.cfg, self.k, self.k.A
        m = A.mark()
        on, _ = self.lay["nfin"]
        ost = [A.alloc([c.D]) for _ in range(2)]
        xn = A.alloc([c.DC, 512])
        oi = 0
        for bi, (t0, n, cond) in enumerate(c.tblocks):
            rs = A.alloc([n])
            self.block_rstd(bi, rs)
            for cc in range(c.DC):
                k.stt(xn[:, cc, 0:n], self.xt(cc, bi), self.vecs[:, on + cc:on + cc + 1], rs, ALU.mult, ALU.mult)
            for t8 in range(n // 128):
                o = ost[oi % 2]
                oi += 1
                for g in range(c.DC // 4):
                    pb = k.bank()
                    for j in range(4):
                        k.tr(pb[:, j * 128:(j + 1) * 128], xn[:, g * 4 + j, t8 * 128:(t8 + 1) * 128], self.ident)
                    k.copy(o[:, g * 512:(g + 1) * 512], pb, eng=("dve" if g % 2 == 0 else "act"))
                r0 = t0 + t8 * 128
                k.dma(self.d["y"][r0:r0 + 128, :], o, is_output=True)
            A.release(A.mark() - n)
        A.release(m)


def prep_core_inputs(cfg, inp, core, consts):
    m = {}
    xs = np.concatenate([inp["x_sample"][core], inp["x_prompt"][2 * core], inp["x_prompt"][2 * core + 1]], 0)
    m["xs"] = np.ascontiguousarray(xs, np.float32)
    m["vecs"] = prep_vecs(cfg, inp, core, consts)
    m["st_r_in"] = np.ascontiguousarray(inp["state_rwkv"][core], np.float32).reshape(-1, 64)
    m["st_d_in"] = np.ascontiguousarray(inp["state_delta"][core], np.float32).reshape(-1, 128)
    m["mod_w"] = inp["mod_w"]
    m["w_in"] = inp["ab_w_in"]
    m["w_out"] = inp["ab_w_out"]
    m["w2"] = inp["rwkv_w2"].reshape(cfg.N_AB, 128, cfg.C_A)
    m["a2"] = inp["rwkv_a2"].reshape(cfg.N_AB, 128, cfg.C_A)
    m["g2"] = inp["rwkv_g2"]
    m["pool_w"] = inp["pool_w"] if cfg.N_POOL > 0 else np.zeros((1, 4, cfg.C_G, cfg.C_G), np.float32)
    m["wg"] = inp["ffn_w_gate"]
    m["wu"] = inp["ffn_w_up"]
    m["wd"] = inp["ffn_w_down"]
    m["ident"] = consts["ident"]
    m["tmask"] = consts["tmask"]
    m["blockones"] = consts["blockones"]
    for tl in sorted({cfg.T_S, cfg.T_P}):
        m["invcnt%d" % tl] = consts["invcnt%d" % tl]
    return m


def run_model(cfg, inp, n_cores, skip_ab=False, trace=False):
    consts = host_consts(cfg)
    inp = {k_: np.asarray(v) for k_, v in inp.items()}
    b = MKB(cfg, skip_ab=skip_ab)
    nc = b.build()
    in_maps = [prep_core_inputs(cfg, inp, core, consts) for core in range(n_cores)]
    res = run_bass_kernel_spmd(nc, in_maps, core_ids=list(range(n_cores)), **({"trace": True} if trace else {}))
    ys, yp, sr, sd = [], [], [], []
    for core in range(n_cores):
        r = res.results[core]
        y = r["y"]
        ys.append(y[:cfg.T_S])
        yp.append(y[cfg.T_S:cfg.T_S + cfg.T_P])
        yp.append(y[cfg.T_S + cfg.T_P:])
        sr.append(r["st_r"].reshape(2, cfg.N_AB, 2, cfg.H_A, 64, 64))
        sd.append(r["st_d"].reshape(2, cfg.N_AB, 2, cfg.H_B, 128, 128))
    out = (np.stack(yp, 0), np.stack(ys, 0), np.concatenate(sr, 0), np.concatenate(sd, 0))
    return out, res, b


def kernel(**inputs):
    cfg = Cfg()
    out, _, _ = run_model(cfg, inputs, 8)
    return tuple(np.ascontiguousarray(o, np.float32) for o in out)


def _ab_methods():
    def wload(self, i, col0, M):
        c, k = self.cfg, self.k
        w = self.wring[self.wri % len(self.wring)]
        self.wri += 1
        src = self.d["w_in"].ap[i].rearrange("(kc p) n -> p kc n", p=128)[:, :, col0:col0 + M]
        k.dma(w[:, :, 0:M], T(src, []), eng="pool")
        return w

    def proj(self, w, M, t0, n, dst, eng="dve"):
        c, k = self.cfg, self.k
        bi = [b for b, (b0, bn, _) in enumerate(c.tblocks) if b0 <= t0 and t0 + n <= b0 + bn][0]
        pb = k.bank()
        for kc in range(c.DC):
            k.mm(pb[0:M, 0:n], w[:, kc, 0:M], T(self.hT.ap[:, kc, t0:t0 + n], [self.hdep[bi]]), start=(kc == 0), stop=(kc == c.DC - 1))
        return pb

    def proj_all(self, w, M, dst):
        c, k = self.cfg, self.k
        for bi, (t0, n, cond) in enumerate(c.tblocks):
            pb = self.proj(w, M, t0, n, None)
            k.copy(dst[0:M, t0:t0 + n], pb[0:M, 0:n], eng=("act" if bi % 2 else "dve"))

    def shift_mix(self, src, dst, M, ci):
        c, k = self.cfg, self.k
        mu7 = self.mu7
        GW = c.GRID_W
        for (s0, sl, grid, cond) in c.seqs:
            S_ = src[0:M, s0:s0 + sl]
            D_ = dst[0:M, s0:s0 + sl]
            k.act(D_, S_, AF.Identity, scale=mu7[0:M, ci, 0:1])
            if not grid:
                k.stt(D_[:, 1:sl], S_[:, 0:sl - 1], mu7[0:M, ci, 1:2], D_[:, 1:sl], ALU.mult, ALU.add)
                k.stt(D_[:, 0:sl - 1], S_[:, 1:sl], mu7[0:M, ci, 2:3], D_[:, 0:sl - 1], ALU.mult, ALU.add)
            else:
                S3 = S_.re("p (r w) -> p r w", w=GW)
                D3 = D_.re("p (r w) -> p r w", w=GW)
                k.stt(D3[:, :, 1:GW], S3[:, :, 0:GW - 1], mu7[0:M, ci, 3:4], D3[:, :, 1:GW], ALU.mult, ALU.add)
                k.stt(D3[:, :, 0:GW - 1], S3[:, :, 1:GW], mu7[0:M, ci, 4:5], D3[:, :, 0:GW - 1], ALU.mult, ALU.add)
                k.stt(D_[:, GW:sl], S_[:, 0:sl - GW], mu7[0:M, ci, 5:6], D_[:, GW:sl], ALU.mult, ALU.add)
                k.stt(D_[:, 0:sl - GW], S_[:, GW:sl], mu7[0:M, ci, 6:7], D_[:, 0:sl - GW], ALU.mult, ALU.add)

    def tri_inverse_gen(self, MRa, MTa, Z, nu, MRb, MTb):
        k = self.k
        cur, oth = MRa, MRb
        ct, ot = MTa, MTb
        idb = self.tmask[0:64, 4:5, :].bc([64, nu, 64])
        k.tt(oth[0:64, :, 64:128], cur[0:64, :, 0:64], idb, ALU.add)
        for u0 in range(0, nu, 8):
            u1 = min(nu, u0 + 8)
            pm = k.bank()
            for u in range(u0, u1):
                k.mm(pm[0:64, (u - u0) * 64:(u - u0 + 1) * 64], ct[0:64, u, :], cur[0:64, u, 0:64])
            k.copy(oth[0:64, u0:u1, 0:64], pm[0:64, 0:(u1 - u0) * 64].re("p (a b) -> p a b", b=64), eng="act")
            yield
            pt = k.bank()
            for u in range(u0, u1):
                k.mm(pt[0:64, (u - u0) * 64:(u - u0 + 1) * 64], cur[0:64, u, 0:64], ct[0:64, u, :])
            k.copy(ot[0:64, u0:u1, :], pt[0:64, 0:(u1 - u0) * 64].re("p (a b) -> p a b", b=64), eng="act")
            yield
        cur, oth = oth, cur
        ct, ot = ot, ct
        for lev in range(1, 5):
            for u0 in range(0, nu, 4):
                u1 = min(nu, u0 + 4)
                pm = k.bank()
                for u in range(u0, u1):
                    k.mm(pm[0:64, (u - u0) * 128:(u - u0 + 1) * 128], ct[0:64, u, :], cur[0:64, u, :])
                p3 = pm[0:64, 0:(u1 - u0) * 128].re("p (a b) -> p a b", b=128)
                if lev < 4:
                    k.copy(oth[0:64, u0:u1, 0:64], p3[:, :, 0:64], eng="act")
                k.tt(oth[0:64, u0:u1, 64:128], cur[0:64, u0:u1, 64:128], p3[:, :, 64:128], ALU.add)
                yield
            for u0 in range(0, nu, 8):
                u1 = min(nu, u0 + 8)
                pt = k.bank()
                for u in range(u0, u1):
                    k.mm(pt[0:64, (u - u0) * 64:(u - u0 + 1) * 64], cur[0:64, u, 0:64], ct[0:64, u, :])
                k.copy(ot[0:64, u0:u1, :], pt[0:64, 0:(u1 - u0) * 64].re("p (a b) -> p a b", b=64), eng="act")
                yield
            cur, oth = oth, cur
            ct, ot = ot, ct
        for u0 in range(0, nu, 8):
            u1 = min(nu, u0 + 8)
            pz = k.bank()
            for u in range(u0, u1):
                k.mm(pz[0:64, (u - u0) * 64:(u - u0 + 1) * 64], ct[0:64, u, :], cur[0:64, u, 64:128])
            k.tt(Z[0:64, u0:u1, :], cur[0:64, u0:u1, 64:128], pz[0:64, 0:(u1 - u0) * 64].re("p (a b) -> p a b", b=64), ALU.add)
            yield

    def interleave(self, main, filler, ratio):
        for _ in main:
            for _r in range(ratio):
                if filler is not None:
                    try:
                        next(filler)
                    except StopIteration:
                        filler = None
        if filler is not None:
            for _ in filler:
                pass

    return dict(wload=wload, proj=proj, proj_all=proj_all, shift_mix=shift_mix, tri_inverse_gen=tri_inverse_gen, interleave=interleave)


for _n, _f in _ab_methods().items():
    setattr(MKB, _n, _f)


def _ab_layer_methods():
    def ab_layer(self, l):
        c, k, A = self.cfg, self.k, self.k.A
        i = l // 2
        DC = c.DC
        CAC = c.C_A // 128
        m0 = A.mark()
        self.hdep = [Dep() for _ in c.tblocks]
        self.hT = A.alloc([DC, c.NT], BF16, deps=self.hdep)
        for bi, (t0, n, cond) in enumerate(c.tblocks):
            self.norm_block(bi, self.AM, 0, lambda cc: T(self.hT.ap[:, cc, t0:t0 + n], [self.hdep[bi]]))
        allx = [self.xdep[cc][bi] for cc in range(DC) for bi in range(len(c.tblocks))]
        k.dma(self.d["xspill"], T(self.xT.ap.rearrange("p a b -> p (a b)"), allx))
        k.S.barrier()
        if DC * c.NT >= 24576:
            AX = Arena(None, None, 0, parent=A, base=self.xT_base, words=DC * c.NT)
        else:
            b0 = A.off
            A.alloc([24576])
            AX = Arena(None, None, 0, parent=A, base=b0, words=24576)
        self.AX = AX
        if l + 1 < c.DEPTH and (l + 1) not in self.mod_done:
            self.bg = self.mod_bg_gen(l + 1, AX.alloc([DC, 64], BF16))
        NMU = c.NMU
        self.mu7 = A.alloc([NMU, 7])
        om, _ = self.lay["mu"]
        mu = self.vecs[:, om + i * NMU:om + (i + 1) * NMU]
        k.ts(self.mu7[:, :, 0], mu, -1.0, ALU.mult, 1.0, ALU.add)
        for j in range(6):
            k.ts(self.mu7[:, :, 1 + j], mu, self.vec("pmask", j), ALU.mult)
        self.wring = [A.alloc([DC, 128], BF16) for _ in range(2)]
        self.wri = 0
        self.odep = [Dep() for _ in range(DC)]
        self.osb = [A.alloc([512], BF16) for _ in range(2)]
        self.osi = 0
        import os
        dbg = os.environ.get("MK_DBG", "")
        self.mark_phase("  rwkv")
        self.rwkv_part(i)
        self.mark_phase("  gdn")
        if "nogdn" in dbg:
            for cc in range(CAC, DC):
                for t0 in range(0, c.NT, 512):
                    n = min(512, c.NT - t0)
                    ob = self.o_tile()
                    k.memset(ob[:, 0:n], 0.0)
                    self.o_store(cc, t0, n, ob)
        else:
            self.gdn_part(i)
        self.bg_step(1000)
        self.mark_phase("  wout")
        A.release(m0)
        k.S.barrier()
        m0 = A.mark()
        k.dma(T(self.xT.ap.rearrange("p a b -> p (a b)"), allx), self.d["xspill"])
        self.oT = A.alloc([DC, c.NT], BF16, deps=self.odep)
        for kc in range(DC):
            k.dma(T(self.oT.ap[:, kc, :], [self.odep[kc]]), T(self.d["oscr"].ap[:, kc, :], [self.odep[kc]]))
        NWO = 2
        wo = [A.alloc([DC, 128], BF16) for _ in range(NWO)]
        wsrc = self.d["w_out"].ap[i].rearrange("(kc p) n -> p kc n", p=128)
        for dc in range(DC):
            w = wo[dc % NWO]
            k.dma(w, T(wsrc[:, :, dc * 128:(dc + 1) * 128], []), eng="pool")
            for bi, (t0, n, cond) in enumerate(c.tblocks):
                po = k.bank()
                for kc in range(DC):
                    k.mm(po[:, 0:n], w[:, kc, :], T(self.oT.ap[:, kc, t0:t0 + n], [self.odep[kc]]), start=(kc == 0), stop=(kc == DC - 1))
                x = self.xt(dc, bi)
                k.stt(x, po[:, 0:n], self.mod(2, dc, cond), x, ALU.mult, ALU.add)
        A.release(m0)
        k.S.barrier()

    def rwkv_part(self, i):
        c, k, A, AX = self.cfg, self.k, self.k.A, self.AX
        DC = c.DC
        CAC = c.C_A // 128
        NT = c.NT
        mA = A.mark()
        praw = A.alloc([NT])
        pmix = A.alloc([NT])
        wlo = A.alloc([NT], BF16)
        alo = A.alloc([NT], BF16)
        glo = A.alloc([NT], BF16)
        glo1 = A.alloc([NT], BF16)
        base = 3 * c.C_A
        for (col, M, ci, dst, fn) in [(base, 128, 3 * CAC, wlo, AF.Tanh), (base + 128, 128, 3 * CAC + 1, alo, AF.Identity),
                                      (base + 256, 128, 3 * CAC + 2, glo, AF.Sigmoid), (base + 384, 32, 3 * CAC + 3, glo1, AF.Sigmoid)]:
            w = self.wload(i, col, M)
            self.proj_all(w, M, praw)
            self.shift_mix(praw, pmix, M, ci)
            k.act(dst[0:M], pmix[0:M], fn)
        w2s = A.alloc([128], BF16)
        a2s = A.alloc([128], BF16)
        g2s = A.alloc([128], BF16)
        g2s1 = A.alloc([128], BF16)
        lnb2 = A.alloc([CAC])
        ol, _ = self.lay["lnb"]
        k.ts(lnb2, self.vecs[:, ol + i * CAC:ol + (i + 1) * CAC], 2.0, ALU.mult)
        CG = 4
        import os
        stopat = int(os.environ.get("MK_STOP", "99"))
        if stopat <= 1:
            A.release(mA)
            return
        for hp in range(CAC):
            mX = AX.mark()
            mA2 = A.mark()
            hs_ = slice(hp * 128, (hp + 1) * 128)
            k.dma(w2s, T(self.d["w2"].ap[i][:, hs_], []), eng="pool")
            k.dma(a2s, T(self.d["a2"].ap[i][:, hs_], []), eng="pool")
            k.dma(g2s, T(self.d["g2"].ap[i, 0:128, hs_], []), eng="pool")
            k.dma(g2s1[0:32], T(self.d["g2"].ap[i, 128:160, hs_], []), eng="pool")
            hs = slice(0, 128)
            rkv = []
            for j in range(3):
                w = self.wload(i, j * c.C_A + hp * 128, 128)
                self.proj_all(w, 128, praw)
                dst = AX.alloc([NT])
                self.shift_mix(praw, dst, 128, j * CAC + hp)
                rkv.append(dst)
            rT, kT, vT = rkv
            vb = AX.alloc([NT], BF16)
            k.copy(vb, vT, eng="pool")
            kk = AX.alloc([NT])
            okk, _ = self.lay["kk"]
            k.ts(kk, kT, self.vecs[:, okk + i * CAC + hp:okk + i * CAC + hp + 1], ALU.mult)
            for bi, (t0, n, cond) in enumerate(c.tblocks):
                k.act(praw[:, t0:t0 + n], kk[:, t0:t0 + n], AF.Square)
                pb = k.bank()
                k.mm(pb[:, 0:n], self.blockones, praw[:, t0:t0 + n])
                k.rsqrt(pmix[:, t0:t0 + n], pb[:, 0:n], 1e-12)
            k.tt(kk, kk, pmix, ALU.mult)
            bonus = AX.alloc([NT])
            ow0, _ = self.lay["w0"]
            oa0, _ = self.lay["a0"]
            oka, _ = self.lay["ka"]
            ork, _ = self.lay["rk"]
            oln, _ = self.lay["lnw"]
            ka_ = self.vecs[:, oka + i * CAC + hp:oka + i * CAC + hp + 1]
            rk_ = self.vecs[:, ork + i * CAC + hp:ork + i * CAC + hp + 1]
            lnw_ = self.vecs[:, oln + i * CAC + hp:oln + i * CAC + hp + 1]
            if stopat <= 2:
                break
            for si, (s0, sl, grid, cond) in enumerate(c.seqs):
                mS = AX.mark()
                mS2 = A.mark()
                nch = sl // 64
                ss = slice(s0, s0 + sl)
                Vtm = AX.alloc([nch, 128], BF16)
                self.to_tm(vb[:, ss], Vtm, nch)
                yn = [AX.alloc([nch, 128], BF16) for _ in range(2)]
                bsum_ps = []
                for d in range(2):
                    mD = AX.mark()
                    mD2 = A.mark()
                    at = AX.alloc([sl], BF16)
                    rt = AX.alloc([sl], BF16)
                    bt = AX.alloc([sl], BF16)
                    kt = AX.alloc([sl], BF16)
                    Pinc = AX.alloc([sl])
                    Btm = AX.alloc([nch, 128], BF16)
                    Ktm = AX.alloc([nch, 128], BF16)
                    mT = AX.mark()
                    lw = AX.alloc([sl])
                    ic = AX.alloc([sl])
                    for o in range(0, sl, 512):
                        n = min(512, sl - o)
                        pb = k.bank()
                        k.mm(pb[:, 0:n], w2s[64 * d:64 * d + 64, hs], wlo[64 * d:64 * d + 64, s0 + o:s0 + o + n])
                        k.act(lw[:, o:o + n], pb[:, 0:n], AF.Sigmoid, bias=self.vecs[:, ow0 + (i * 2 + d) * CAC + hp:ow0 + (i * 2 + d) * CAC + hp + 1])
                        pb2 = k.bank()
                        k.mm(pb2[:, 0:n], a2s[64 * d:64 * d + 64, hs], alo[64 * d:64 * d + 64, s0 + o:s0 + o + n])
                        k.act(ic[:, o:o + n], pb2[:, 0:n], AF.Sigmoid, bias=self.vecs[:, oa0 + (i * 2 + d) * CAC + hp:oa0 + (i * 2 + d) * CAC + hp + 1])
                    k.ts(lw, lw, -0.6065306597126334, ALU.mult)
                    cumI = AX.alloc([sl])
                    self.chunk_cumsum(lw, cumI, sl, d, AX)
                    cumE = AX.alloc([sl])
                    k.tt(cumE, cumI, lw, ALU.subtract)
                    k.act(Pinc, cumI, AF.Exp)
                    k.act(cumE, cumE, AF.Exp)
                    k.act(cumI, cumI, AF.Exp, scale=-1.0)
                    Pexc, Pinv = cumE, cumI
                    kd = AX.alloc([sl])
                    k.ts(kd, ic, 1.0, ALU.subtract, ka_, ALU.mult)
                    k.stt(kd, kd, 1.0, kT[:, ss], ALU.add, ALU.mult)
                    k.stt(at, kk[:, ss], -1.0, Pexc, ALU.mult, ALU.mult)
                    k.tt(rt, rT[:, ss], Pinc, ALU.mult)
                    k.tt(lw, kk[:, ss], ic, ALU.mult)
                    k.tt(bt, lw, Pinv, ALU.mult)
                    k.tt(kt, kd, Pinv, ALU.mult)
                    k.stt(kd, rT[:, ss], rk_, kd, ALU.mult, ALU.mult)
                    rkr = kd
                    for o in range(0, sl, 512):
                        n = min(512, sl - o)
                        pb = k.bank()
                        k.mm(pb[:, 0:n], self.blockones, rkr[:, o:o + n])
                        if d == 0:
                            k.tt(bonus[:, s0 + o:s0 + o + n], pb[:, 0:n], vT[:, s0 + o:s0 + o + n], ALU.mult)
                        else:
                            k.tt(ic[:, o:o + n], pb[:, 0:n], vT[:, s0 + o:s0 + o + n], ALU.mult)
                            k.tt(bonus[:, s0 + o:s0 + o + n], bonus[:, s0 + o:s0 + o + n], ic[:, o:o + n], ALU.add)
                    if stopat <= 3:
                        break
                    self.to_tm(bt, Btm, nch)
                    self.to_tm(kt, Ktm, nch)
                    AX.release(mT)
                    Sf = A.alloc([64])
                    Sb = A.alloc([64], BF16)
                    self.rwkv_init_state(i, d, hp, grid, Sf, Sb)
                    order = list(range(nch)) if d == 0 else list(range(nch - 1, -1, -1))
                    if stopat <= 4:
                        break
                    self.rwkv_seqdir(d, order, at, rt, bt, kt, Pinc, Vtm, Btm, Ktm, Sf, Sb, yn[d])
                    if not grid:
                        self.rwkv_store_state(i, d, hp, si, Sf)
                    A.release(mD2)
                    AX.release(mD)
                if stopat <= 6:
                    break
                for n0 in range(0, nch, 8):
                    n1 = min(nch, n0 + 8)
                    pb = k.bank()
                    for n_ in range(n0, n1):
                        cs = slice((n_ - n0) * 64, (n_ - n0 + 1) * 64)
                        k.mm(pb[:, cs], yn[0][0:64, n_, :], self.identb[0:64, 0:64], start=True, stop=False)
                        k.mm(pb[:, cs], yn[1][0:64, n_, :], self.identb[0:64, 0:64], start=False, stop=True)
                    ts_ = slice(s0 + n0 * 64, s0 + n1 * 64)
                    nn = (n1 - n0) * 64
                    k.stt(praw[:, 0:nn], pb[:, 0:nn], lnw_, bonus[:, ts_], ALU.mult, ALU.add)
                    pg = k.bank()
                    k.mm(pg[:, 0:nn], g2s[:, hs], glo[:, ts_], start=True, stop=False)
                    k.mm(pg[:, 0:nn], g2s1[0:32, hs], glo1[0:32, ts_], start=False, stop=True)
                    ob = self.o_tile()
                    k.stt(ob[:, 0:nn], praw[:, 0:nn], lnb2[:, hp:hp + 1], pg[:, 0:nn], ALU.add, ALU.mult)
                    self.o_store(hp, s0 + n0 * 64, nn, ob)
                self.bg_step(4)
                A.release(mS2)
                AX.release(mS)
            A.release(mA2)
            AX.release(mX)
        A.release(mA)

    def dump(self, name, t, parts, free):
        import os
        if "dump" not in os.environ.get("MK_DBG", ""):
            return
        dt_ = t.ap.dtype
        dr = T(self.nc.dram_tensor(name, [parts, free], dt_, kind="Internal").ap(), [Dep()])
        self.k.dma(dr, t)

    def o_tile(self):
        t = self.osb[self.osi % 2]
        self.osi += 1
        return t

    def o_store(self, cc, t0, n, ob):
        self.k.dma(T(self.d["oscr"].ap[:, cc, t0:t0 + n], [self.odep[cc]]), ob[:, 0:n])

    def to_tm(self, src, dst, nch):
        k = self.k
        for n0 in range(0, nch, 8):
            n1 = min(nch, n0 + 8)
            pb = k.bank()
            pbb = T(pb.ap.bitcast(BF16), pb.deps, True)
            for n_ in range(n0, n1):
                k.tr(pbb[0:64, (n_ - n0) * 128:(n_ - n0 + 1) * 128], src[:, n_ * 64:(n_ + 1) * 64], self.identb)
            k.copy(dst[0:64, n0:n1, :], pbb[0:64, 0:(n1 - n0) * 128].re("p (a b) -> p a b", b=128), eng="act")

    def chunk_cumsum(self, src, dst, sl, d, AR):
        k = self.k
        P = src.ap.shape[0]
        m = AR.mark()
        mask = AR.alloc([sl])
        k.memset(mask[0:P], 1.0, eng="pool")
        k.memset(mask[0:P].re("p (a b) -> p a b", b=64)[:, :, 0:1], 0.0, eng="pool")
        k.scan(dst, mask[0:P], src, 0.0, ALU.mult, ALU.add)
        if d == 1:
            nch = sl // 64
            d3 = dst.re("p (a b) -> p a b", b=64)
            tot = d3[:, :, 63:64].bc([P, nch, 64])
            k.tt(mask[0:P].re("p (a b) -> p a b", b=64), tot, d3, ALU.subtract)
            k.tt(dst, mask[0:P], src, ALU.add)
        AR.release(m)

    def rwkv_init_state(self, i, d, hp, grid, Sf, Sb):
        c, k, A = self.cfg, self.k, self.k.A
        import os
        if not grid or "noinit" in os.environ.get("MK_DBG", ""):
            k.memset(Sf, 0.0)
            k.memset(Sb, 0.0)
            return
        m = A.mark()
        st2 = A.alloc([2, 64])
        for e in range(2):
            r0 = ((i * 2 + d) * c.H_A + 2 * hp + e) * 64
            if "xsrc" in os.environ.get("MK_DBG", ""):
                k.dma(st2[0:64, e, :], T(self.d["xs"].ap[0:64, e * 64:(e + 1) * 64], []))
            else:
                k.dma(st2[0:64, e, :], T(self.d["st_r_in"].ap[r0:r0 + 64, :], []))
        pb = k.bank()
        k.mm(pb[:, 0:64], st2[0:64].re("p a b -> p (a b)"), self.ident[0:64, 0:64])
        k.copy(Sf, pb[:, 0:64])
        k.copy(Sb, pb[:, 0:64], eng="act")
        A.release(m)

    def rwkv_store_state(self, i, d, hp, si, Sf):
        c, k, A = self.cfg, self.k, self.k.A
        m = A.mark()
        pb = k.bank()
        k.mm(pb[0:64, 0:128], Sf, self.ident)
        o = A.alloc([128])
        k.copy(o[0:64], pb[0:64, 0:128])
        for e in range(2):
            r0 = ((((si - 1) * c.N_AB + i) * 2 + d) * c.H_A + 2 * hp + e) * 64
            k.dma(T(self.d["st_r"].ap[r0:r0 + 64, :], self.d["st_r"].deps), o[0:64, e * 64:(e + 1) * 64], is_output=True)
        A.release(m)

    def rwkv_seqdir(self, d, order, at, rt, bt, kt, Pinc, Vtm, Btm, Ktm, Sf, Sb, yn):
        c, k, A = self.cfg, self.k, self.k.A
        m = A.mark()
        CG = 2
        groups = [order[g0:g0 + CG] for g0 in range(0, len(order), CG)]
        nu = 2 * CG
        mS, mI = (0, 1) if d == 0 else (2, 3)
        mST = 2 if d == 0 else 0
        sets = [dict(MR=A.alloc([nu, 128]), MTn=A.alloc([nu, 64]), Z=A.alloc([nu, 64])) for _ in range(2)]
        Mb = A.alloc([nu, 128])
        MTb = A.alloc([nu, 64])
        Aak = A.alloc([nu, 64], BF16)
        Arb = A.alloc([nu, 64], BF16)
        Ark = A.alloc([nu, 64], BF16)
        W1s = A.alloc([2, 64])
        Ub = A.alloc([2, 64], BF16)
        st6 = A.alloc([2, 6])
        mv = A.alloc([2, 2])
        rstd = A.alloc([2])
        SP = A.alloc([64])

        def amat(dst, L, R, mask, grp):
            ng = len(grp)
            pb = k.bank()
            for u in range(2 * ng):
                e, gi = divmod(u, ng)
                cs = slice(grp[gi] * 64, grp[gi] * 64 + 64)
                ps = slice(64 * e, 64 * e + 64)
                k.mm(pb[0:64, u * 64:(u + 1) * 64], L[ps, cs], R[ps, cs])
            k.tt(dst[0:64, 0:2 * ng, :], pb[0:64, 0:2 * ng * 64].re("p (a b) -> p a b", b=64),
                 self.tmask[0:64, mask:mask + 1, :].bc([64, 2 * ng, 64]), ALU.mult)

        def prep(grp, S_):
            amat(S_["MR"][:, :, 0:64], bt, at, mS, grp)
            yield
            amat(S_["MTn"], at, bt, mST, grp)
            yield
            ngu = 2 * len(grp)
            yield from self.tri_inverse_gen(S_["MR"][:, 0:ngu, :], S_["MTn"][:, 0:ngu, :], S_["Z"][:, 0:ngu, :], ngu, Mb[:, 0:ngu, :], MTb[:, 0:ngu, :])

        def chunks(grp, S_):
            ng = len(grp)
            Z = S_["Z"]
            amat(Aak, kt, at, mS, grp)
            amat(Arb, bt, rt, mI, grp)
            amat(Ark, kt, rt, mI, grp)
            for gi, n_ in enumerate(grp):
                cs = slice(n_ * 64, n_ * 64 + 64)
                pc = Pinc[:, n_ * 64 + 63:n_ * 64 + 64] if d == 0 else Pinc[:, n_ * 64:n_ * 64 + 1]
                k.act(SP, Sf, AF.Identity, scale=pc)
                pw = k.bank()
                for e in range(2):
                    ps = slice(64 * e, 64 * e + 64)
                    u = e * ng + gi
                    k.mm(pw[0:64, e * 64:(e + 1) * 64], at[ps, cs], Sb[ps, :], start=True, stop=False)
                    k.mm(pw[0:64, e * 64:(e + 1) * 64], Aak[0:64, u, :], Vtm[0:64, n_, e * 64:(e + 1) * 64], start=False, stop=True)
                k.copy(W1s[0:64].re("p a b -> p (a b)"), pw[0:64, 0:128])
                yield
                pu = k.bank()
                for e in range(2):
                    u = e * ng + gi
                    k.mm(pu[0:64, e * 64:(e + 1) * 64], Z[0:64, u, :], W1s[0:64, e, :])
                k.copy(Ub[0:64].re("p a b -> p (a b)"), pu[0:64, 0:128], eng="act")
                yield
                py = k.bank()
                pd = k.bank()
                for e in range(2):
                    ps = slice(64 * e, 64 * e + 64)
                    u = e * ng + gi
                    yo = py[0:64, e * 64:(e + 1) * 64]
                    k.mm(yo, rt[ps, cs], Sb[ps, :], start=True, stop=False)
                    k.mm(yo, Arb[0:64, u, :], Ub[0:64, e, :], start=False, stop=False)
                    k.mm(yo, Ark[0:64, u, :], Vtm[0:64, n_, e * 64:(e + 1) * 64], start=False, stop=True)
                    do = pd[ps, 0:64]
                    k.mm(do, Btm[0:64, n_, ps], Ub[0:64, e, :], start=True, stop=False)
                    k.mm(do, Ktm[0:64, n_, ps], Vtm[0:64, n_, e * 64:(e + 1) * 64], start=False, stop=True)
                k.stt(Sf, pd[:, 0:64], pc, SP, ALU.mult, ALU.add)
                k.copy(Sb, Sf, eng="act")
                for e in range(2):
                    k.bnstats(st6[0:64, e, :], py[0:64, e * 64:(e + 1) * 64])
                    k.bnaggr(mv[0:64, e, :], st6[0:64, e, :])
                k.rsqrt(rstd[0:64], mv[0:64, :, 1], 64e-5)
                for e in range(2):
                    k.ts(yn[0:64, n_, e * 64:(e + 1) * 64], py[0:64, e * 64:(e + 1) * 64], mv[0:64, e, 0:1], ALU.subtract, rstd[0:64, e:e + 1], ALU.mult)
                yield
        for _ in prep(groups[0], sets[0]):
            pass
        for g, grp in enumerate(groups):
            filler = prep(groups[g + 1], sets[(g + 1) % 2]) if g + 1 < len(groups) else None
            self.interleave(chunks(grp, sets[g % 2]), filler, 3)
        A.release(m)

    return dict(ab_layer=ab_layer, rwkv_part=rwkv_part, to_tm=to_tm, dump=dump, o_tile=o_tile, o_store=o_store, chunk_cumsum=chunk_cumsum, rwkv_init_state=rwkv_init_state,
                rwkv_store_state=rwkv_store_state, rwkv_seqdir=rwkv_seqdir)


for _n, _f in _ab_layer_methods().items():
    setattr(MKB, _n, _f)


def _gdn_methods():
    def gdn_part(self, i):
        c, k, A, AX = self.cfg, self.k, self.k.A, self.AX
        DC, NT, H_B = c.DC, c.NT, c.H_B
        CAC = c.C_A // 128
        CBC = c.C_B // 128
        PB0 = c.C_PA
        M2 = 2 * H_B
        mA = A.mark()
        mX0 = AX.mark()
        praw = A.alloc([NT])
        pmix = A.alloc([NT])
        NCH = NT // 64
        colsT = A.alloc([NCH, 96])
        mR = A.mark()
        rows = A.alloc([NT])
        graw = A.alloc([NT])
        k.memset(rows, 0.0, eng="pool")
        oal, _ = self.lay["alog"]
        odt, _ = self.lay["dtb"]
        nea = A.alloc([1])
        k.act(nea[0:M2], self.vecs[0:M2, oal + i:oal + i + 1], AF.Exp)
        k.ts(nea[0:M2], nea[0:M2], -1.0, ALU.mult)
        wbeta = self.wload(i, PB0 + 4 * c.C_B, M2)
        for bi, (t0, n, cond) in enumerate(c.tblocks):
            pb = self.proj(wbeta, M2, t0, n, None)
            k.act(rows[64:64 + M2, t0:t0 + n], pb[0:M2, 0:n], AF.Sigmoid)
        walpha = self.wload(i, PB0 + 4 * c.C_B + M2, M2)
        for bi, (t0, n, cond) in enumerate(c.tblocks):
            pb = self.proj(walpha, M2, t0, n, None)
            x = praw[0:M2, t0:t0 + n]
            ax = pmix[0:M2, t0:t0 + n]
            k.act(x, pb[0:M2, 0:n], AF.Identity, bias=self.vecs[0:M2, odt + i:odt + i + 1])
            k.ts(ax, x, -1.0, ALU.mult)
            k.tt(ax, ax, x, ALU.max)
            k.act(ax, ax, AF.Exp, scale=-1.0)
            k.act(ax, ax, AF.Ln, bias=1.0)
            k.stt(x, x, 0.0, ax, ALU.max, ALU.add)
            k.ts(graw[0:M2, t0:t0 + n], x, nea[0:M2], ALU.mult)
        for (s0, sl, grid, cond) in c.seqs:
            ss = slice(s0, s0 + sl)
            self.chunk_cumsum(graw[0:M2, ss], rows[0:M2, ss], sl, 0, A)
            self.chunk_cumsum(graw[0:M2, ss], rows[32:32 + M2, ss], sl, 1, A)
        k.dma(self.d["rowsc"], rows[0:96, :])
        for n0 in range(0, NCH, 5):
            n1 = min(NCH, n0 + 5)
            pb = k.bank()
            for n_ in range(n0, n1):
                k.tr(pb[0:64, (n_ - n0) * 96:(n_ - n0 + 1) * 96], rows[0:96, n_ * 64:(n_ + 1) * 64], self.ident[0:96, 0:96])
            k.copy(colsT[0:64, n0:n1, :], pb[0:64, 0:(n1 - n0) * 96].re("p (a b) -> p a b", b=96))
        A.release(mR)
        ocw, _ = self.lay["convw"]
        ogn, _ = self.lay["gnw"]
        for h in range(H_B):
            mX = AX.mark()
            mA2 = A.mark()
            qb = AX.alloc([NT], BF16)
            kb = AX.alloc([NT], BF16)
            vb = AX.alloc([NT], BF16)
            zs_ = AX.alloc([NT])
            mQ = AX.mark()
            fm = []
            for j in range(4):
                w = self.wload(i, PB0 + j * c.C_B + h * 128, 128)
                self.proj_all(w, 128, praw)
                if j < 3:
                    cw = lambda tap: self.vecs[:, ocw + (i * 3 + tap) * 3 * CBC + j * CBC + h:ocw + (i * 3 + tap) * 3 * CBC + j * CBC + h + 1]
                    for (s0, sl, grid, cond) in c.seqs:
                        S_ = praw[:, s0:s0 + sl]
                        D_ = pmix[:, s0:s0 + sl]
                        k.act(D_, S_, AF.Identity, scale=cw(1))
                        k.stt(D_[:, 1:sl], S_[:, 0:sl - 1], cw(0), D_[:, 1:sl], ALU.mult, ALU.add)
                        k.stt(D_[:, 0:sl - 1], S_[:, 1:sl], cw(2), D_[:, 0:sl - 1], ALU.mult, ALU.add)
                    dst = AX.alloc([NT])
                    k.act(dst, pmix, AF.Silu)
                else:
                    dst = zs_
                    k.act(dst, praw, AF.Silu)
                fm.append(dst)
            qf, kf, vf, zs = fm
            k.copy(vb, vf, eng="pool")
            for (src, dstb, sc) in ((qf, qb, 128.0 ** -0.5), (kf, kb, 1.0)):
                for bi, (t0, n, cond) in enumerate(c.tblocks):
                    k.act(praw[:, t0:t0 + n], src[:, t0:t0 + n], AF.Square)
                    pb = k.bank()
                    k.mm(pb[:, 0:n], self.onesf, praw[:, t0:t0 + n])
                    k.rsqrt(pmix[:, t0:t0 + n], pb[:, 0:n], 1e-6)
                    k.stt(dstb[:, t0:t0 + n], src[:, t0:t0 + n], sc, pmix[:, t0:t0 + n], ALU.mult, ALU.mult)
            AX.release(mQ)
            for si, (s0, sl, grid, cond) in enumerate(c.seqs):
                mS = AX.mark()
                mS2 = A.mark()
                nch = sl // 64
                co = s0 // 64
                ss = slice(s0, s0 + sl)
                Ktm = AX.alloc([nch, 128], BF16)
                Vtm = AX.alloc([nch, 128], BF16)
                self.to_tm(kb[:, ss], Ktm, nch)
                self.to_tm(vb[:, ss], Vtm, nch)
                QK = AX.alloc([nch, 64])
                KK = AX.alloc([nch, 64])
                for (dst, R_) in ((QK, qb), (KK, kb)):
                    for n0 in range(0, nch, 8):
                        n1 = min(nch, n0 + 8)
                        pb = k.bank()
                        for n_ in range(n0, n1):
                            cs = slice(s0 + n_ * 64, s0 + n_ * 64 + 64)
                            k.mm(pb[0:64, (n_ - n0) * 64:(n_ - n0 + 1) * 64], kb[:, cs], R_[:, cs])
                        k.copy(dst[0:64, n0:n1, :], pb[0:64, 0:(n1 - n0) * 64].re("p (a b) -> p a b", b=64), eng="act")
                osum = AX.alloc([nch, 128])
                for d in range(2):
                    mD = AX.mark()
                    mD2 = A.mark()
                    r = d * H_B + h
                    grow = (0 if d == 0 else 32) + r
                    Gbc = AX.alloc([sl])
                    Bbc = AX.alloc([sl])
                    k.dma(Gbc, T(self.d["rowsc"].ap[grow:grow + 1, ss].partition_broadcast(128), self.d["rowsc"].deps))
                    k.dma(Bbc, T(self.d["rowsc"].ap[64 + r:65 + r, ss].partition_broadcast(128), self.d["rowsc"].deps))
                    Gcol = colsT[0:64, co:co + nch, grow]
                    Bcol = colsT[0:64, co:co + nch, 64 + r]
                    eG = AX.alloc([sl])
                    k.act(eG, Gbc, AF.Exp)
                    kbg = AX.alloc([sl], BF16)
                    qg = AX.alloc([sl], BF16)
                    tmpf = AX.alloc([sl])
                    k.tt(tmpf, Bbc, eG, ALU.mult)
                    k.tt(kbg, kb[:, ss], tmpf, ALU.mult)
                    k.tt(qg, qb[:, ss], eG, ALU.mult)
                    G3 = Gbc[0:64].re("p (n c) -> p n c", c=64)
                    B3 = Bbc[0:64].re("p (n c) -> p n c", c=64)
                    Gc3 = Gcol.v(lambda a: a.unsqueeze(2)).bc([64, nch, 64])
                    Bc3 = Bcol.v(lambda a: a.unsqueeze(2)).bc([64, nch, 64])
                    mInc, mStr, mStrT = (1, 0, 2) if d == 0 else (3, 2, 0)
                    tm_ = lambda idx: self.tmask[0:64, idx:idx + 1, :].bc([64, nch, 64])
                    DmT = AX.alloc([nch, 64])
                    k.tt(DmT[0:64], G3, Gc3, ALU.subtract)
                    k.ts(DmT[0:64], DmT[0:64], 0.0, ALU.min)
                    k.act(DmT[0:64], DmT[0:64], AF.Exp)
                    attnT = AX.alloc([nch, 64], BF16)
                    MRn = AX.alloc([nch, 128])
                    Mn = MRn[:, :, 0:64]
                    MTn = AX.alloc([nch, 64])
                    Z = AX.alloc([nch, 64])
                    t3 = tmpf[0:64].re("p (n c) -> p n c", c=64)
                    k.tt(t3, DmT[0:64], tm_(mInc), ALU.mult)
                    k.tt(attnT[0:64], QK[0:64], t3, ALU.mult)
                    k.tt(t3, DmT[0:64], tm_(mStr), ALU.mult)
                    k.tt(t3, t3, B3, ALU.mult)
                    k.stt(Mn[0:64], KK[0:64], -1.0, t3, ALU.mult, ALU.mult)
                    k.tt(t3, Gc3, G3, ALU.subtract)
                    k.ts(t3, t3, 0.0, ALU.min)
                    k.act(t3, t3, AF.Exp)
                    k.tt(t3, t3, tm_(mStrT), ALU.mult)
                    k.tt(t3, t3, Bc3, ALU.mult)
                    k.stt(MTn[0:64], KK[0:64], -1.0, t3, ALU.mult, ALU.mult)
                    egl = A.alloc([nch])
                    Glast = G3[:, :, 63] if d == 0 else G3[:, :, 0]
                    k.tt(egl[0:64], Glast, Gcol, ALU.subtract)
                    k.act(egl[0:64], egl[0:64], AF.Exp)
                    kdtm = AX.alloc([nch, 128], BF16)
                    vbtm = AX.alloc([nch, 128])
                    k.tt(kdtm[0:64], Ktm[0:64], egl[0:64].v(lambda a: a.unsqueeze(2)).bc([64, nch, 128]), ALU.mult)
                    k.tt(vbtm[0:64], Vtm[0:64], Bcol.v(lambda a: a.unsqueeze(2)).bc([64, nch, 128]), ALU.mult)
                    Sf = A.alloc([128])
                    Sb = A.alloc([128], BF16)
                    if grid:
                        r0 = ((i * 2 + d) * H_B + h) * 128
                        k.dma(Sf, T(self.d["st_d_in"].ap[r0:r0 + 128, :], []))
                    else:
                        k.memset(Sf, 0.0)
                    k.copy(Sb, Sf, eng="act")
                    W1s = A.alloc([128])
                    vnb = A.alloc([128], BF16)
                    order = list(range(nch)) if d == 0 else list(range(nch - 1, -1, -1))
                    CGD = 4
                    ggrps = [order[g0:g0 + CGD] for g0 in range(0, nch, CGD)]
                    Mbt = A.alloc([CGD, 128])
                    MTbt = A.alloc([CGD, 64])

                    def gprep(grp):
                        u0, u1 = min(grp), max(grp) + 1
                        yield from self.tri_inverse_gen(MRn[:, u0:u1, :], MTn[:, u0:u1, :], Z[:, u0:u1, :], u1 - u0, Mbt[:, 0:u1 - u0, :], MTbt[:, 0:u1 - u0, :])

                    def gchunks(grp):
                        for n_ in grp:
                            cs = slice(n_ * 64, n_ * 64 + 64)
                            pw = k.bank()
                            k.mm(pw[0:64, 0:128], kbg[:, cs], Sb)
                            k.tt(W1s[0:64], vbtm[0:64, n_, :], pw[0:64, 0:128], ALU.subtract)
                            yield
                            pv = k.bank()
                            k.mm(pv[0:64, 0:128], Z[0:64, n_, :], W1s[0:64])
                            k.copy(vnb[0:64], pv[0:64, 0:128], eng="act")
                            yield
                            po = k.bank()
                            k.mm(po[0:64, 0:128], qg[:, cs], Sb, start=True, stop=False)
                            k.mm(po[0:64, 0:128], attnT[0:64, n_, :], vnb[0:64], start=False, stop=True)
                            pd = k.bank()
                            k.mm(pd[:, 0:128], kdtm[0:64, n_, :], vnb[0:64])
                            gc = eG[:, n_ * 64 + 63:n_ * 64 + 64] if d == 0 else eG[:, n_ * 64:n_ * 64 + 1]
                            k.stt(Sf, Sf, gc, pd[:, 0:128], ALU.mult, ALU.add)
                            k.copy(Sb, Sf, eng="act")
                            if d == 0:
                                k.copy(osum[0:64, n_, :], po[0:64, 0:128], eng="pool" if False else "dve")
                            else:
                                k.tt(osum[0:64, n_, :], osum[0:64, n_, :], po[0:64, 0:128], ALU.add)
                            yield
                    for _ in gprep(ggrps[0]):
                        pass
                    for g, grp in enumerate(ggrps):
                        filler = gprep(ggrps[g + 1]) if g + 1 < len(ggrps) else None
                        self.interleave(gchunks(grp), filler, 2)
                    if not grid:
                        r0 = (((si - 1) * c.N_AB + i) * 2 + d) * H_B * 128 + h * 128
                        k.dma(T(self.d["st_d"].ap[r0:r0 + 128, :], self.d["st_d"].deps), Sf, is_output=True)
                    A.release(mD2)
                    AX.release(mD)
                osq = AX.alloc([nch, 128])
                ssq = A.alloc([nch])
                for n_ in range(nch):
                    k.act_acc(osq[0:64, n_, :], osum[0:64, n_, :], AF.Square, ssq[0:64, n_:n_ + 1])
                k.rsqrt(ssq[0:64], ssq[0:64], 1e-6, scale=1.0 / 128)
                k.tt(osq[0:64], osum[0:64], ssq[0:64].v(lambda a: a.unsqueeze(2)).bc([64, nch, 128]), ALU.mult)
                for n0 in range(0, nch, 8):
                    n1 = min(nch, n0 + 8)
                    pb = k.bank()
                    for n_ in range(n0, n1):
                        k.mm(pb[:, (n_ - n0) * 64:(n_ - n0 + 1) * 64], osq[0:64, n_, :], self.ident[0:64, 0:64])
                    nn = (n1 - n0) * 64
                    ts_ = slice(s0 + n0 * 64, s0 + n1 * 64)
                    ob = self.o_tile()
                    k.stt(ob[:, 0:nn], pb[:, 0:nn], self.vecs[:, ogn + i:ogn + i + 1], zs[:, ts_], ALU.mult, ALU.mult)
                    self.o_store(CAC + h, s0 + n0 * 64, nn, ob)
                self.bg_step(4)
                A.release(mS2)
                AX.release(mS)
            A.release(mA2)
            AX.release(mX)
        A.release(mA)
        AX.release(mX0)

    return dict(gdn_part=gdn_part)


AX_X = AX.X
for _n, _f in _gdn_methods().items():
    setattr(MKB, _n, _f)
```

```python
import numpy as np
from contextlib import ExitStack
import concourse.bass as bass
import concourse.mybir as mybir
from concourse.bass_utils import run_bass_kernel_spmd

F32 = mybir.dt.float32
BF16 = mybir.dt.bfloat16
AF = mybir.ActivationFunctionType
ALU = mybir.AluOpType
AX = mybir.AxisListType


class Dep:
    __slots__ = ("w", "r")

    def __init__(self):
        self.w = None
        self.r = {}


class Sched:
    ENG = ("pe", "dve", "act", "pool", "sp")

    def __init__(self, nc, es, n_dma_sems=40, trust=("pe",)):
        self.nc = nc
        self.prog = {k: [] for k in self.ENG}
        self.sems = {}
        for k in self.ENG:
            self.sems[k] = es.enter_context(nc.semaphore("s_" + k))
        self.cnt = {k: 0 for k in self.ENG}
        self.known = {k: {} for k in self.ENG}
        self.trust = set(trust)
        self.ndma = n_dma_sems
        for i in range(n_dma_sems):
            self.sems[("d", i)] = es.enter_context(nc.semaphore("s_d%d" % i))
        self.dcnt = [0] * n_dma_sems
        self.drr = 0
        self.drr_sw = 0
        self.out_events = []

    def _need(self, eng, waits, ev):
        if ev is None:
            return
        k, v = ev
        if k == eng and eng in self.trust:
            return
        if self.known[eng].get(k, 0) >= v:
            return
        if waits.get(k, 0) < v:
            waits[k] = v

    def _collect(self, eng, reads, writes):
        waits = {}
        for d in reads:
            self._need(eng, waits, d.w)
        for d in writes:
            self._need(eng, waits, d.w)
            for k, v in d.r.items():
                self._need(eng, waits, (k, v))
        return waits

    def _commit(self, eng, waits, ev, reads, writes):
        for k, v in waits.items():
            self.known[eng][k] = v
        for d in writes:
            d.w = ev
            d.r = {}
        k, v = ev
        for d in reads:
            if d.r.get(k, 0) < v:
                d.r[k] = v

    def op(self, eng, fn, reads=(), writes=(), force=()):
        waits = self._collect(eng, reads, writes)
        for (fk, fv) in force:
            if self.known[eng].get(fk, 0) < fv and waits.get(fk, 0) < fv:
                waits[fk] = fv
        self.cnt[eng] += 1
        ev = (eng, self.cnt[eng])
        self.prog[eng].append((list(waits.items()), fn, eng, 1))
        self._commit(eng, waits, ev, reads, writes)
        return ev

    def dma(self, eng, fn, reads=(), writes=(), is_output=False):
        waits = self._collect(eng, reads, writes)
        half = self.ndma // 2
        if eng == "pool":
            s = half + self.drr_sw
            self.drr_sw = (self.drr_sw + 1) % (self.ndma - half)
        else:
            s = self.drr
            self.drr = (self.drr + 1) % half
        key = ("d", s)
        if self.dcnt[s] > 0:
            self._need(eng, waits, (key, self.dcnt[s]))
        self.dcnt[s] += 16
        ev = (key, self.dcnt[s])
        self.prog[eng].append((list(waits.items()), fn, key, 16))
        self._commit(eng, waits, ev, reads, writes)
        if is_output:
            self.out_events.append(ev)
        return ev

    def barrier(self):
        snap = {k: self.cnt[k] for k in self.ENG if self.cnt[k] > 0}
        for i in range(self.ndma):
            if self.dcnt[i] > 0:
                snap[("d", i)] = self.dcnt[i]
        for eng in self.ENG:
            waits = {}
            for k, v in snap.items():
                if k == eng and eng in self.trust:
                    continue
                if self.known[eng].get(k, 0) < v:
                    waits[k] = v
            if waits:
                self.prog[eng].append((list(waits.items()), None, None, 0))
                for k, v in waits.items():
                    self.known[eng][k] = v

    def finish(self):
        waits = {}
        for ev in self.out_events:
            self._need("sp", waits, ev)
        self.prog["sp"].append((list(waits.items()), None, None, 0))

    def emit(self, block):
        sems = self.sems

        def mk(engname):
            items = self.prog[engname]

            def body(e):
                for waits, fn, inck, incv in items:
                    for k, v in waits:
                        e.wait_ge(sems[k], v)
                    if fn is not None:
                        fn(e).then_inc(sems[inck], incv)
            return body
        block.tensor(mk("pe"))
        block.vector(mk("dve"))
        block.scalar(mk("act"))
        block.gpsimd(mk("pool"))
        block.sync(mk("sp"))


class T:
    __slots__ = ("ap", "deps", "ps")

    def __init__(self, ap, deps, ps=False):
        self.ap = ap
        self.deps = deps
        self.ps = ps

    def __getitem__(self, k):
        return T(self.ap[k], self.deps, self.ps)

    def v(self, fn):
        return T(fn(self.ap), self.deps, self.ps)

    def re(self, s, **kw):
        return T(self.ap.rearrange(s, **kw), self.deps, self.ps)

    def bc(self, shape):
        return T(self.ap.to_broadcast(list(shape)), self.deps, self.ps)


class Arena:
    def __init__(self, nc, es, kbytes, parent=None, base=0, words=0):
        if parent is None:
            self.words = kbytes * 256
            self.t = es.enter_context(nc.sbuf_tensor("arena", [128, self.words], F32))
            self.base = 0
        else:
            self.words = words
            self.t = parent.t
            self.base = base
        self.off = 0
        self.peak = 0
        self.live = []
        self.ghosts = []

    def alloc(self, shape, dtype=F32, deps=None):
        n = int(np.prod(shape))
        w = n if dtype == F32 else (n + 1) // 2
        assert self.off + w <= self.words, "arena overflow %d + %d > %d" % (self.off, w, self.words)
        s0, e0 = self.off, self.off + w
        v = self.t[:, self.base + s0:self.base + e0]
        self.off += w
        self.peak = max(self.peak, self.off)
        if dtype != F32:
            v = v.bitcast(dtype)
            if 2 * w != n:
                v = v[:, 0:n]
        if len(shape) == 2:
            v = v.rearrange("p (a b) -> p a b", a=shape[0])
        elif len(shape) == 3:
            v = v.rearrange("p (a b c) -> p a b c", a=shape[0], b=shape[1])
        deps = [Dep()] if deps is None else deps
        keep = []
        for (gs, ge, gd) in self.ghosts:
            if gs < e0 and s0 < ge:
                for od in gd:
                    evs = list(od.r.items())
                    if od.w is not None:
                        evs.append(od.w)
                    for nd in deps:
                        for kk_, vv_ in evs:
                            if nd.r.get(kk_, 0) < vv_:
                                nd.r[kk_] = vv_
                if gs >= s0 and ge <= e0:
                    continue
            keep.append((gs, ge, gd))
        self.ghosts = keep
        self.live.append((s0, e0, deps))
        return T(v, deps)

    def mark(self):
        return self.off

    def release(self, m):
        keep = []
        for (s0, e0, dd) in self.live:
            if e0 > m:
                self.ghosts.append((max(s0, m), e0, dd))
                if s0 < m:
                    keep.append((s0, m, dd))
            else:
                keep.append((s0, e0, dd))
        self.live = keep
        self.off = m


def _sc(x):
    return x.ap if isinstance(x, T) else x


def _dp(*xs):
    out = []
    for x in xs:
        if isinstance(x, T):
            out.extend(x.deps)
    return out


def _wr(out, *ins):
    w = list(out.deps)
    for x in ins:
        if isinstance(x, T) and x.ps:
            w.extend(x.deps)
    return w


class K:
    def __init__(self, nc, es, arena_kb):
        self.nc = nc
        self.S = Sched(nc, es)
        self.A = Arena(nc, es, arena_kb)
        self.psum = es.enter_context(nc.psum_tensor("ps", [128, 8, 512], F32))
        self.pdeps = [Dep() for _ in range(8)]
        self.pi = 0
        self.rg = {}

    def bank(self):
        i = self.pi % 8
        self.pi += 1
        return T(self.psum[:, i, :], [self.pdeps[i]], True)

    def tt(self, out, a, b, op, eng="dve"):
        return self.S.op(eng, lambda e: e.tensor_tensor(out=out.ap, in0=a.ap, in1=b.ap, op=op), reads=_dp(a, b), writes=_wr(out, a, b))

    def ts(self, out, a, s1, op0, s2=None, op1=None, eng="dve"):
        if op1 is None:
            return self.S.op(eng, lambda e: e.tensor_scalar(out=out.ap, in0=a.ap, scalar1=_sc(s1), scalar2=None, op0=op0), reads=_dp(a, s1), writes=_wr(out, a))
        return self.S.op(eng, lambda e: e.tensor_scalar(out=out.ap, in0=a.ap, scalar1=_sc(s1), scalar2=_sc(s2), op0=op0, op1=op1), reads=_dp(a, s1, s2), writes=_wr(out, a))

    def stt(self, out, a, s, b, op0, op1):
        return self.S.op("dve", lambda e: e.scalar_tensor_tensor(out=out.ap, in0=a.ap, scalar=_sc(s), in1=b.ap, op0=op0, op1=op1), reads=_dp(a, s, b), writes=_wr(out, a, b))

    def act(self, out, a, func, scale=1.0, bias=0.0):
        return self.S.op("act", lambda e: e.activation(out=out.ap, in_=a.ap, func=func, scale=_sc(scale), bias=_sc(bias)), reads=_dp(a, scale, bias), writes=_wr(out, a))

    def act_acc(self, out, a, func, acc):
        return self.S.op("act", lambda e: e.activation(out=out.ap, in_=a.ap, func=func, accum_out=acc.ap), reads=_dp(a), writes=_wr(out, a) + acc.deps)

    def copy(self, out, a, eng="dve"):
        if eng == "act":
            return self.S.op("act", lambda e: e.copy(out=out.ap, in_=a.ap), reads=_dp(a), writes=_wr(out, a))
        return self.S.op(eng, lambda e: e.tensor_copy(out=out.ap, in_=a.ap), reads=_dp(a), writes=_wr(out, a))

    def memset(self, out, val, eng="dve"):
        return self.S.op(eng, lambda e: e.memset(out.ap, val), writes=out.deps)

    def recip(self, out, a):
        return self.S.op("dve", lambda e: e.reciprocal(out=out.ap, in_=a.ap), reads=_dp(a), writes=_wr(out, a))

    def scan(self, out, d0, d1, init, op0, op1):
        return self.S.op("dve", lambda e: e.tensor_tensor_scan(out=out.ap, data0=d0.ap, data1=d1.ap, initial=init, op0=op0, op1=op1), reads=_dp(d0, d1), writes=_wr(out, d0, d1))

    def _rowgrp_force(self, out, lhsT):
        rg = (lhsT.ap.base_partition(), lhsT.ap.shape[0])
        force = []
        for d in out.deps:
            prev = self.rg.get(id(d))
            if prev is not None and prev[0] != rg:
                force.append(prev[1])
        return rg, force

    def mm(self, out, lhsT, rhs, start=True, stop=True):
        rg, force = self._rowgrp_force(out, lhsT)
        ev = self.S.op("pe", lambda e: e.matmul(out.ap, lhsT=lhsT.ap, rhs=rhs.ap, start=start, stop=stop), reads=_dp(lhsT, rhs), writes=out.deps, force=force)
        for d in out.deps:
            self.rg[id(d)] = (rg, ev)
        return ev

    def tr(self, out, a, ident):
        rg, force = self._rowgrp_force(out, a)
        ev = self.S.op("pe", lambda e: e.transpose(out=out.ap, in_=a.ap, identity=ident.ap), reads=_dp(a, ident), writes=out.deps, force=force)
        for d in out.deps:
            self.rg[id(d)] = (rg, ev)
        return ev

    def bnstats(self, out, a):
        return self.S.op("dve", lambda e: e.bn_stats(out=out.ap, in_=a.ap), reads=_dp(a), writes=_wr(out, a))

    def bnaggr(self, out, a):
        return self.S.op("dve", lambda e: e.bn_aggr(out=out.ap, in_=a.ap), reads=_dp(a), writes=_wr(out, a))

    def dma(self, out, a, eng="sp", is_output=False):
        return self.S.dma(eng, lambda e: e.dma_start(out=out.ap, in_=a.ap), reads=_dp(a), writes=out.deps, is_output=is_output)

    def rsqrt(self, out, a, eps, scale=1.0):
        self.act(out, a, AF.Sqrt, scale=scale, bias=eps)
        self.recip(out, out)


class Cfg:
    def __init__(self, D=2048, DEPTH=4, T_S=1024, T_P=256, DFF=None):
        self.D = D
        self.DC = D // 128
        self.DEPTH = DEPTH
        self.N_AB = (DEPTH + 1) // 2
        self.N_POOL = DEPTH // 2
        self.T_S = T_S
        self.T_P = T_P
        self.NT = T_S + 2 * T_P
        self.seqs = [(0, T_S, True, 1), (T_S, T_P, False, 0), (T_S + T_P, T_P, False, 0)]
        self.DFF = DFF if DFF is not None else ((8 * D + 3 * 256 - 1) // (3 * 256)) * 256
        self.FCN = self.DFF // 128
        self.C_A = D // 2
        self.H_A = self.C_A // 64
        self.NHP = self.C_A // 128
        self.C_B = D // 2
        self.H_B = self.C_B // 128
        self.C_PA = 3 * self.C_A + 416
        self.C_PB = 4 * self.C_B + 4 * self.H_B
        self.C_IN = self.C_PA + self.C_PB
        self.NMU = (self.C_PA + 127) // 128
        self.C_G = D // 4
        self.GRID_W = 64
        self.tblocks = []
        for (t0, tl, g, c) in [(0, T_S, True, 1), (T_S, 2 * T_P, False, 0)]:
            o = 0
            while o < tl:
                n = min(512, tl - o)
                self.tblocks.append((t0 + o, n, c))
                o += n


def host_consts(cfg):
    c = {}
    c["ident"] = np.eye(128, dtype=np.float32)
    p = np.arange(128)
    pm = np.zeros((128, 8), np.float32)
    pm[:, 0] = (p % 2 == 0)
    pm[:, 1] = (p % 2 == 1)
    for j in range(4):
        pm[:, 2 + j] = (p % 4 == j)
    c["pmask"] = pm
    j = np.arange(64)[:, None]
    i = np.arange(64)[None, :]
    tm = np.zeros((64, 8, 64), np.float32)
    tm[:, 0] = (i > j)
    tm[:, 1] = (i >= j)
    tm[:, 2] = (i < j)
    tm[:, 3] = (i <= j)
    tm[:, 4] = (i == j)
    c["tmask"] = np.concatenate([tm, np.zeros((64, 8, 64), np.float32)], 0)
    bo = np.zeros((128, 128), np.float32)
    bo[:64, :64] = 1
    bo[64:, 64:] = 1
    c["blockones"] = bo
    for tl in sorted({cfg.T_S, cfg.T_P}):
        pos = np.arange(tl)
        ic = np.zeros((4, tl), np.float32)
        for gi, win in enumerate((2, 4, 8, 16)):
            lo = np.clip(pos - win // 2, 0, tl)
            hi = np.clip(pos - win // 2 + win, 0, tl)
            ic[gi] = 1.0 / (hi - lo)
        c["invcnt%d" % tl] = np.broadcast_to(ic[None], (128, 4, tl)).copy()
    return c


def vec_layout(cfg):
    CAC = cfg.C_A // 128
    CBC = cfg.C_B // 128
    items = [("cond", cfg.DC * 2), ("modb", cfg.DEPTH * 6 * cfg.DC), ("nmix", cfg.DEPTH * cfg.DC),
             ("nffn", cfg.DEPTH * cfg.DC), ("nfin", cfg.DC), ("mu", cfg.N_AB * cfg.NMU),
             ("w0", cfg.N_AB * 2 * CAC), ("a0", cfg.N_AB * 2 * CAC), ("kk", cfg.N_AB * CAC), ("ka", cfg.N_AB * CAC),
             ("lnw", cfg.N_AB * CAC), ("lnb", cfg.N_AB * CAC), ("rk", cfg.N_AB * CAC),
             ("convw", cfg.N_AB * 3 * 3 * CBC), ("gnw", cfg.N_AB), ("alog", cfg.N_AB), ("dtb", cfg.N_AB),
             ("pscale", max(1, cfg.N_POOL) * cfg.DC), ("pmask", 8)]
    lay = {}
    o = 0
    for n, sz in items:
        lay[n] = (o, sz)
        o += sz
    return lay, o


def chunked(v):
    v = np.asarray(v, np.float32)
    lead = v.shape[:-1]
    n = v.shape[-1] // 128
    v = v.reshape(lead + (n, 128))
    v = np.moveaxis(v, -1, 0)
    return np.ascontiguousarray(v).reshape(128, -1)


def prep_vecs(cfg, inp, core, consts):
    lay, nv = vec_layout(cfg)
    V = np.zeros((128, nv), np.float32)

    def put(name, arr):
        o, sz = lay[name]
        assert arr.shape == (128, sz), (name, arr.shape, sz)
        V[:, o:o + sz] = arr
    cond = np.stack([chunked(inp["c_ctx"]), chunked(inp["c"][core])], -1).reshape(128, -1)
    put("cond", cond)
    put("modb", chunked(inp["mod_b"].reshape(cfg.DEPTH, 6, cfg.D)))
    put("nmix", chunked(inp["norm_mix_w"]))
    put("nffn", chunked(inp["norm_ffn_w"]))
    put("nfin", chunked(inp["norm_final_w"]))
    mu = np.zeros((cfg.N_AB, cfg.NMU * 128), np.float32)
    mu[:, :cfg.C_PA] = inp["rwkv_mu"]
    put("mu", chunked(mu))
    put("w0", chunked(inp["rwkv_w0"]))
    put("a0", chunked(inp["rwkv_a0"]))
    put("kk", chunked(inp["rwkv_k_k"]))
    put("ka", chunked(inp["rwkv_k_a"]))
    put("lnw", chunked(inp["rwkv_ln_w"]))
    put("lnb", chunked(inp["rwkv_ln_b"]))
    put("rk", chunked(inp["rwkv_r_k"].reshape(cfg.N_AB, cfg.C_A)))
    put("convw", chunked(inp["gdn_conv_w"]))
    put("gnw", np.ascontiguousarray(np.asarray(inp["gdn_norm_w"], np.float32).T))
    al = np.zeros((128, cfg.N_AB), np.float32)
    al[:2 * cfg.H_B] = np.asarray(inp["gdn_a_log"], np.float32).reshape(cfg.N_AB, -1).T
    put("alog", al)
    db = np.zeros((128, cfg.N_AB), np.float32)
    db[:2 * cfg.H_B] = np.asarray(inp["gdn_dt_bias"], np.float32).reshape(cfg.N_AB, -1).T
    put("dtb", db)
    if cfg.N_POOL > 0:
        put("pscale", chunked(inp["pool_scale"]))
    put("pmask", consts["pmask"])
    return V


class MKB:
    def __init__(self, cfg, skip_ab=False, arena_kb=206):
        self.cfg = cfg
        self.skip_ab = skip_ab
        self.arena_kb = arena_kb

    def declare(self, nc):
        c = self.cfg
        d = {}

        def inp(name, shape, dt=F32):
            d[name] = T(nc.dram_tensor(name, list(shape), dt, kind="ExternalInput").ap(), [])

        def outp(name, shape):
            d[name] = T(nc.dram_tensor(name, list(shape), F32, kind="ExternalOutput").ap(), [Dep()])
        lay, nv = vec_layout(c)
        self.lay = lay
        inp("xs", [c.NT, c.D])
        inp("vecs", [128, nv])
        inp("st_r_in", [c.N_AB * 2 * c.H_A * 64, 64])
        inp("st_d_in", [c.N_AB * 2 * c.H_B * 128, 128])
        inp("mod_w", [c.DEPTH, c.D, 6 * c.D])
        inp("w_in", [c.N_AB, c.D, c.C_IN])
        inp("w_out", [c.N_AB, c.D, c.D])
        inp("w2", [c.N_AB, 128, c.C_A])
        inp("a2", [c.N_AB, 128, c.C_A])
        inp("g2", [c.N_AB, 160, c.C_A])
        inp("pool_w", [max(1, c.N_POOL), 4, c.C_G, c.C_G])
        inp("wg", [c.DEPTH, c.D, c.DFF])
        inp("wu", [c.DEPTH, c.D, c.DFF])
        inp("wd", [c.DEPTH, c.DFF, c.D])
        inp("ident", [128, 128])
        inp("tmask", [128, 8, 64])
        inp("blockones", [128, 128])
        for tl in sorted({c.T_S, c.T_P}):
            inp("invcnt%d" % tl, [128, 4, tl])
        outp("y", [c.NT, c.D])
        outp("st_r", [2 * c.N_AB * 2 * c.H_A * 64, 64])
        outp("st_d", [2 * c.N_AB * 2 * c.H_B * 128, 128])
        if not self.skip_ab:
            d["xspill"] = T(nc.dram_tensor("xspill", [128, c.DC * c.NT], F32, kind="Internal").ap(), [Dep()])
            d["rowsc"] = T(nc.dram_tensor("rowsc", [96, c.NT], F32, kind="Internal").ap(), [Dep()])
            d["oscr"] = T(nc.dram_tensor("oscr", [128, c.DC, c.NT], BF16, kind="Internal").ap(), [])
        self.d = d

    def vec(self, name, idx, n=1):
        o, sz = self.lay[name]
        assert idx + n <= sz
        return self.vecs[:, o + idx:o + idx + n]

    def build(self):
        c = self.cfg
        nc = bass.Bass("TRN2", target_bir_lowering=False)
        self.nc = nc
        self.declare(nc)
        with ExitStack() as es:
            k = K(nc, es, self.arena_kb)
            self.k = k
            A = k.A
            d = self.d
            lay, nv = vec_layout(c)
            self.xT_base = A.off
            self.xT = A.alloc([c.DC, c.NT], deps=[])
            self.xdep = [[Dep() for _ in c.tblocks] for _ in range(c.DC)]
            self.vecs = A.alloc([nv])
            self.ident = A.alloc([128])
            self.identb = A.alloc([128], BF16)
            self.onesb = A.alloc([128], BF16)
            self.onesf = A.alloc([128])
            self.blockones = A.alloc([128])
            self.tmask = A.alloc([8, 64])
            self.modT = A.alloc([6 * c.DC, 2])
            self.dvec = A.alloc([16 + 4 * c.DC * 2])
            k.dma(self.vecs, d["vecs"])
            k.dma(self.ident, d["ident"])
            k.dma(self.blockones, d["blockones"])
            k.dma(self.tmask, d["tmask"])
            k.copy(self.identb, self.ident)
            k.memset(self.onesb, 1.0)
            k.memset(self.onesf, 1.0)
            self.condact = A.alloc([c.DC, 2], BF16)
            k.act(self.condact.re("p a b -> p (a b)"), self.vec("cond", 0, c.DC * 2), AF.Silu)
            self.marks = []
            self.mark_phase = lambda nm: self.marks.append((nm, k.S.cnt["pe"]))
            self.load_x()
            import os
            dbg = os.environ.get("MK_DBG", "")
            for l in range(c.DEPTH):
                self.mark_phase("L%d mod" % l)
                if "nomod" in dbg:
                    k.memset(self.modT, 0.0)
                    k.memset(self.dvec, 1.0)
                    self.AM = self.dvec[:, 16:16 + c.DC * 2].re("p (a b) -> p a b", b=2)
                    self.AFv = self.dvec[:, 16 + c.DC * 2:16 + c.DC * 4].re("p (a b) -> p a b", b=2)
                else:
                    self.compute_mod(l)
                self.mark_phase("L%d mixer" % l)
                if l % 2 == 0:
                    if not self.skip_ab:
                        self.ab_layer(l)
                elif "nopool" not in dbg:
                    self.pool_layer(l)
                self.mark_phase("L%d ffn" % l)
                if "noffn" not in dbg:
                    self.ffn_layer(l)
            self.mark_phase("final")
            self.final_store()
            self.mark_phase("end")
            k.S.finish()
            self.stats = dict(arena_peak_kb=A.peak / 256, instrs={e: len(v) for e, v in k.S.prog.items()})
            with nc.Block() as block:
                k.S.emit(block)
        return nc

    def xt(self, cidx, bi):
        t0, n, _ = self.cfg.tblocks[bi]
        return T(self.xT.ap[:, cidx, t0:t0 + n], [self.xdep[cidx][bi]])

    def xt_deps(self, t0, n):
        out = []
        for bi, (b0, bn, _) in enumerate(self.cfg.tblocks):
            if b0 < t0 + n and t0 < b0 + bn:
                out.append(bi)
        return out

    def load_x(self):
        c, k, A = self.cfg, self.k, self.k.A
        m = A.mark()
        stg = [A.alloc([c.D]) for _ in range(2)]
        for tt in range(c.NT // 128):
            s = stg[tt % 2]
            k.dma(s, self.d["xs"][tt * 128:(tt + 1) * 128, :])
            bi = [i for i, (b0, bn, _) in enumerate(c.tblocks) if b0 <= tt * 128 < b0 + bn][0]
            for g in range(c.DC // 4):
                pb = k.bank()
                for j in range(4):
                    cc = g * 4 + j
                    k.tr(pb[:, j * 128:(j + 1) * 128], s[:, cc * 128:(cc + 1) * 128], self.ident)
                dst = T(self.xT.ap[:, g * 4:(g + 1) * 4, tt * 128:(tt + 1) * 128], [self.xdep[cc][bi] for cc in range(g * 4, g * 4 + 4)])
                k.copy(dst, pb.re("p (a b) -> p a b", a=4), eng=("dve" if g % 2 == 0 else "act"))
        A.release(m)
        k.S.barrier()

    def compute_mod(self, l):
        c, k, A = self.cfg, self.k, self.k.A
        m = A.mark()
        CB = 512
        nblk = (6 * c.D) // CB
        wb = [A.alloc([c.DC, CB], BF16) for _ in range(2)]
        src = self.d["mod_w"].ap[l].rearrange("(kc p) n -> p kc n", p=128)
        pb = k.bank()
        for b in range(nblk):
            w = wb[b % 2]
            k.dma(w, T(src[:, :, b * CB:(b + 1) * CB], []), eng="pool")
            for j in range(CB // 128):
                oc = b * (CB // 128) + j
                for kc in range(c.DC):
                    k.mm(pb[:, oc * 2:oc * 2 + 2], w[:, kc, j * 128:(j + 1) * 128], self.condact[:, kc, :], start=(kc == 0), stop=(kc == c.DC - 1))
        n6 = 6 * c.DC
        ob, _ = self.lay["modb"]
        mb = self.vecs[:, ob + l * n6: ob + (l + 1) * n6]
        k.tt(self.modT, pb[:, 0:n6 * 2].re("p (a b) -> p a b", b=2), mb.v(lambda a: a.unsqueeze(2)).bc([128, n6, 2]), ALU.add)
        DC = c.DC
        self.AM = self.dvec[:, 16:16 + DC * 2].re("p (a b) -> p a b", b=2)
        self.AFv = self.dvec[:, 16 + DC * 2:16 + DC * 4].re("p (a b) -> p a b", b=2)
        on, _ = self.lay["nmix"]
        of, _ = self.lay["nffn"]
        nm = self.vecs[:, on + l * DC:on + (l + 1) * DC].v(lambda a: a.unsqueeze(2)).bc([128, DC, 2])
        nf = self.vecs[:, of + l * DC:of + (l + 1) * DC].v(lambda a: a.unsqueeze(2)).bc([128, DC, 2])
        k.stt(self.AM, self.modT[:, 1 * DC:2 * DC, :], 1.0, nm, ALU.add, ALU.mult)
        k.stt(self.AFv, self.modT[:, 4 * DC:5 * DC, :], 1.0, nf, ALU.add, ALU.mult)
        A.release(m)
        k.S.barrier()

    def mod(self, j, cidx, cond):
        return self.modT[:, j * self.cfg.DC + cidx, cond:cond + 1]

    def block_rstd(self, bi, out):
        c, k, A = self.cfg, self.k, self.k.A
        t0, n, _ = c.tblocks[bi]
        m = A.mark()
        sq = [A.alloc([n], BF16) for _ in range(2)]
        pb = k.bank()
        for cc in range(c.DC):
            k.act(sq[cc % 2], self.xt(cc, bi), AF.Square)
            k.mm(pb[:, 0:n], self.onesb, sq[cc % 2], start=(cc == 0), stop=(cc == c.DC - 1))
        k.rsqrt(out, pb[:, 0:n], 1e-6, scale=1.0 / c.D)
        A.release(m)

    def norm_block(self, bi, Avec, shift_j, dst_fn, rs=None):
        c, k, A = self.cfg, self.k, self.k.A
        t0, n, cond = c.tblocks[bi]
        m = A.mark()
        if rs is None:
            rs = A.alloc([n])
            self.block_rstd(bi, rs)
        tmp = [A.alloc([n]) for _ in range(2)]
        for cc in range(c.DC):
            k.tt(tmp[cc % 2], self.xt(cc, bi), rs, ALU.mult)
            k.act(dst_fn(cc), tmp[cc % 2], AF.Identity, scale=Avec[:, cc, cond:cond + 1], bias=self.mod(shift_j, cc, cond))
        A.release(m)

    def ffn_layer(self, l):
        c, k, A = self.cfg, self.k, self.k.A
        m = A.mark()
        hdep = [Dep() for _ in c.tblocks]
        hT = A.alloc([c.DC, c.NT], BF16, deps=hdep)
        for bi, (t0, n, cond) in enumerate(c.tblocks):
            self.norm_block(bi, self.AFv, 3, lambda cc: T(hT.ap[:, cc, t0:t0 + n], [hdep[bi]]))
        import os
        dbg = os.environ.get("MK_DBG", "")
        if "h2x" in dbg:
            for bi, (t0, n, cond) in enumerate(c.tblocks):
                for cc in range(c.DC):
                    k.copy(self.xt(cc, bi), T(hT.ap[:, cc, t0:t0 + n], [hdep[bi]]))
            A.release(m)
            k.S.barrier()
            return
        FB = 2 if c.FCN % 2 == 0 else 1
        NW = 2
        NWD = 1
        wgb = [A.alloc([c.DC, FB * 128], BF16) for _ in range(NW)]
        wub = [A.alloc([c.DC, FB * 128], BF16) for _ in range(NW)]
        wdb = [A.alloc([FB, c.D], BF16) for _ in range(NWD)]
        adep = [[Dep() for _ in c.tblocks] for _ in range(FB)]
        actT = A.alloc([FB, c.NT], BF16, deps=[x_ for r_ in adep for x_ in r_])
        sg = [A.alloc([512]) for _ in range(2)]
        wg_v = self.d["wg"].ap[l].rearrange("(kc p) f -> p kc f", p=128)
        wu_v = self.d["wu"].ap[l].rearrange("(kc p) f -> p kc f", p=128)
        wd_v = self.d["wd"].ap[l].rearrange("(fc p) d -> p fc d", p=128)
        si = 0
        for fb in range(c.FCN // FB):
            w = fb % NW
            fs = slice(fb * FB * 128, (fb + 1) * FB * 128)
            k.dma(wgb[w], T(wg_v[:, :, fs], []), eng="pool")
            k.dma(wub[w], T(wu_v[:, :, fs], []), eng="pool")
            wd_ = wdb[fb % NWD]
            k.dma(wd_, T(wd_v[:, fb * FB:(fb + 1) * FB, :], []), eng="pool")
            if "w2x" in dbg:
                for bi, (t0, n, cond) in enumerate(c.tblocks):
                    for cc in range(c.DC):
                        k.copy(self.xt(cc, bi), wgb[w][:, cc, 0:n])
                break
            for fc in range(FB):
                for bi, (t0, n, cond) in enumerate(c.tblocks):
                    pg = k.bank()
                    pu = k.bank()
                    h = lambda kc: T(hT.ap[:, kc, t0:t0 + n], [hdep[bi]])
                    for kc in range(c.DC):
                        k.mm(pg[:, 0:n], wgb[w][:, kc, fc * 128:(fc + 1) * 128], h(kc), start=(kc == 0), stop=(kc == c.DC - 1))
                    for kc in range(c.DC):
                        k.mm(pu[:, 0:n], wub[w][:, kc, fc * 128:(fc + 1) * 128], h(kc), start=(kc == 0), stop=(kc == c.DC - 1))
                    s = sg[si % 2]
                    si += 1
                    k.act(s[:, 0:n], pg[:, 0:n], AF.Silu)
                    k.tt(T(actT.ap[:, fc, t0:t0 + n], [adep[fc][bi]]), s[:, 0:n], pu[:, 0:n], ALU.mult)
                    if "g2x" in dbg and fb == 0 and fc < c.DC:
                        k.copy(self.xt(fc, bi), pg[:, 0:n])
                    if "a2x" in dbg and fb == 0 and fc < c.DC:
                        k.copy(self.xt(fc, bi), T(actT.ap[:, fc, t0:t0 + n], [adep[fc][bi]]))
            if "g2x" in dbg or "a2x" in dbg:
                break
            for dc in range(c.DC):
                for bi, (t0, n, cond) in enumerate(c.tblocks):
                    po = k.bank()
                    for fc in range(FB):
                        k.mm(po[:, 0:n], wd_[:, fc, dc * 128:(dc + 1) * 128], T(actT.ap[:, fc, t0:t0 + n], [adep[fc][bi]]), start=(fc == 0), stop=(fc == FB - 1))
                    x = self.xt(dc, bi)
                    k.stt(x, po[:, 0:n], self.mod(5, dc, cond), x, ALU.mult, ALU.add)
        A.release(m)
        k.S.barrier()

    def pool_layer(self, l):
        c, k, A = self.cfg, self.k, self.k.A
        i = l // 2
        m = A.mark()
        GC = c.DC // 4
        rs = [A.alloc([n]) for (t0, n, cond) in c.tblocks]
        for bi in range(len(c.tblocks)):
            self.block_rstd(bi, rs[bi])
        gs = A.alloc([c.DC, 2])
        op_, _ = self.lay["pscale"]
        ps = self.vecs[:, op_ + i * c.DC:op_ + (i + 1) * c.DC].v(lambda a: a.unsqueeze(2)).bc([128, c.DC, 2])
        k.tt(gs, self.modT[:, 2 * c.DC:3 * c.DC, :], ps, ALU.mult)
        PAD = 8
        invc = {}
        for tl in sorted({c.T_S, c.T_P}):
            invc[tl] = A.alloc([4, tl])
            k.dma(invc[tl], self.d["invcnt%d" % tl])
        wp = [A.alloc([GC, c.C_G], BF16) for _ in range(2)]
        for gi in range(4):
            win = (2, 4, 8, 16)[gi]
            w = wp[gi % 2]
            k.dma(w, T(self.d["pool_w"].ap[i, gi].rearrange("(kc p) e -> p kc e", p=128), []), eng="pool")
            pT = A.alloc([GC, c.NT], BF16) if gi == 0 else pT
            for g in range(GC):
                cc = gi * GC + g
                hp = A.alloc([c.NT + 2 * PAD * len(c.seqs)]) if (gi == 0 and g == 0) else hp
                wa = A.alloc([c.NT + 2 * PAD * len(c.seqs)]) if (gi == 0 and g == 0) else wa
                wb_ = A.alloc([c.NT + 2 * PAD * len(c.seqs)]) if (gi == 0 and g == 0) else wb_
                k.memset(hp, 0.0, eng="pool")
                for bi, (t0, n, cond) in enumerate(c.tblocks):
                    for si_, (s0, sl, grid, scond) in enumerate(c.seqs):
                        a0 = max(t0, s0)
                        a1 = min(t0 + n, s0 + sl)
                        if a0 >= a1:
                            continue
                        off = PAD * (2 * si_ + 1)
                        tmp = A.alloc([a1 - a0])
                        k.tt(tmp, T(self.xT.ap[:, cc, a0:a1], [self.xdep[cc][bi]]), rs[bi][:, a0 - t0:a1 - t0], ALU.mult)
                        k.act(hp[:, off + a0:off + a1], tmp, AF.Identity, scale=self.AM[:, cc, cond:cond + 1], bias=self.mod(0, cc, cond))
                        A.release(A.mark() - (a1 - a0))
                for si_, (s0, sl, grid, scond) in enumerate(c.seqs):
                    off = PAD * (2 * si_ + 1)
                    lo = off + s0 - PAD
                    L = sl + 2 * PAD
                    H = hp[:, lo:lo + L]
                    W1 = wa[:, lo:lo + L]
                    W2 = wb_[:, lo:lo + L]
                    k.tt(W1[:, 1:L], H[:, 0:L - 1], H[:, 1:L], ALU.add)
                    cur = W1
                    nxt = W2
                    if win >= 4:
                        k.tt(nxt[:, 2:L - 1], cur[:, 1:L - 2], cur[:, 3:L], ALU.add)
                        cur, nxt = nxt, cur
                    if win >= 8:
                        k.tt(nxt[:, 4:L - 3], cur[:, 2:L - 5], cur[:, 6:L - 1], ALU.add)
                        cur, nxt = nxt, cur
                    if win >= 16:
                        k.tt(nxt[:, 8:L - 7], cur[:, 4:L - 11], cur[:, 12:L - 3], ALU.add)
                        cur, nxt = nxt, cur
                    k.tt(nxt[:, PAD:PAD + sl], cur[:, PAD:PAD + sl], invc[sl][:, gi, :], ALU.mult)
                    k.tt(pT[:, g, s0:s0 + sl], nxt[:, PAD:PAD + sl], H[:, PAD:PAD + sl], ALU.subtract)
            for e in range(GC):
                ec = gi * GC + e
                for bi, (t0, n, cond) in enumerate(c.tblocks):
                    po = k.bank()
                    for g in range(GC):
                        k.mm(po[:, 0:n], w[:, g, e * 128:(e + 1) * 128], pT[:, g, t0:t0 + n], start=(g == 0), stop=(g == GC - 1))
                    x = self.xt(ec, bi)
                    k.stt(x, po[:, 0:n], gs[:, ec, cond:cond + 1], x, ALU.mult, ALU.add)
        A.release(m)
        k.S.barrier()

    def final_store(self):
        c, k, A = self.cfg, self.k, self.k.A
        m = A.mark()
        on, _ = self.lay["nfin"]
        ost = [A.alloc([c.D]) for _ in range(2)]
        xn = A.alloc([c.DC, 512])
        oi = 0
        for bi, (t0, n, cond) in enumerate(c.tblocks):
            rs = A.alloc([n])
            self.block_rstd(bi, rs)
            for cc in range(c.DC):
                k.stt(xn[:, cc, 0:n], self.xt(cc, bi), self.vecs[:, on + cc:on + cc + 1], rs, ALU.mult, ALU.mult)
            for t8 in range(n // 128):
                o = ost[oi % 2]
                oi += 1
                for g in range(c.DC // 4):
                    pb = k.bank()
                    for j in range(4):
                        k.tr(pb[:, j * 128:(j + 1) * 128], xn[:, g * 4 + j, t8 * 128:(t8 + 1) * 128], self.ident)
                    k.copy(o[:, g * 512:(g + 1) * 512], pb, eng=("dve" if g % 2 == 0 else "act"))
                r0 = t0 + t8 * 128
                k.dma(self.d["y"][r0:r0 + 128, :], o, is_output=True)
            A.release(A.mark() - n)
        A.release(m)


def prep_core_inputs(cfg, inp, core, consts):
    m = {}
    xs = np.concatenate([inp["x_sample"][core], inp["x_prompt"][2 * core], inp["x_prompt"][2 * core + 1]], 0)
    m["xs"] = np.ascontiguousarray(xs, np.float32)
    m["vecs"] = prep_vecs(cfg, inp, core, consts)
    m["st_r_in"] = np.ascontiguousarray(inp["state_rwkv"][core], np.float32).reshape(-1, 64)
    m["st_d_in"] = np.ascontiguousarray(inp["state_delta"][core], np.float32).reshape(-1, 128)
    m["mod_w"] = inp["mod_w"]
    m["w_in"] = inp["ab_w_in"]
    m["w_out"] = inp["ab_w_out"]
    m["w2"] = inp["rwkv_w2"].reshape(cfg.N_AB, 128, cfg.C_A)
    m["a2"] = inp["rwkv_a2"].reshape(cfg.N_AB, 128, cfg.C_A)
    m["g2"] = inp["rwkv_g2"]
    m["pool_w"] = inp["pool_w"] if cfg.N_POOL > 0 else np.zeros((1, 4, cfg.C_G, cfg.C_G), np.float32)
    m["wg"] = inp["ffn_w_gate"]
    m["wu"] = inp["ffn_w_up"]
    m["wd"] = inp["ffn_w_down"]
    m["ident"] = consts["ident"]
    m["tmask"] = consts["tmask"]
    m["blockones"] = consts["blockones"]
    for tl in sorted({cfg.T_S, cfg.T_P}):
        m["invcnt%d" % tl] = consts["invcnt%d" % tl]
    return m


def run_model(cfg, inp, n_cores, skip_ab=False, trace=False):
    consts = host_consts(cfg)
    inp = {k_: np.asarray(v) for k_, v in inp.items()}
    b = MKB(cfg, skip_ab=skip_ab)
    nc = b.build()
    in_maps = [prep_core_inputs(cfg, inp, core, consts) for core in range(n_cores)]
    res = run_bass_kernel_spmd(nc, in_maps, core_ids=list(range(n_cores)), **({"trace": True} if trace else {}))
    ys, yp, sr, sd = [], [], [], []
    for core in range(n_cores):
        r = res.results[core]
        y = r["y"]
        ys.append(y[:cfg.T_S])
        yp.append(y[cfg.T_S:cfg.T_S + cfg.T_P])
        yp.append(y[cfg.T_S + cfg.T_P:])
        sr.append(r["st_r"].reshape(2, cfg.N_AB, 2, cfg.H_A, 64, 64))
        sd.append(r["st_d"].reshape(2, cfg.N_AB, 2, cfg.H_B, 128, 128))
    out = (np.stack(yp, 0), np.stack(ys, 0), np.concatenate(sr, 0), np.concatenate(sd, 0))
    return out, res, b


def kernel(**inputs):
    cfg = Cfg()
    out, _, _ = run_model(cfg, inputs, 8)
    return tuple(np.ascontiguousarray(o, np.float32) for o in out)


def _ab_methods():
    def wload(self, i, col0, M):
        c, k = self.cfg, self.k
        w = self.wring[self.wri % len(self.wring)]
        self.wri += 1
        src = self.d["w_in"].ap[i].rearrange("(kc p) n -> p kc n", p=128)[:, :, col0:col0 + M]
        k.dma(w[:, :, 0:M], T(src, []), eng="pool")
        return w

    def proj(self, w, M, t0, n, dst, eng="dve"):
        c, k = self.cfg, self.k
        bi = [b for b, (b0, bn, _) in enumerate(c.tblocks) if b0 <= t0 and t0 + n <= b0 + bn][0]
        pb = k.bank()
        for kc in range(c.DC):
            k.mm(pb[0:M, 0:n], w[:, kc, 0:M], T(self.hT.ap[:, kc, t0:t0 + n], [self.hdep[bi]]), start=(kc == 0), stop=(kc == c.DC - 1))
        return pb

    def proj_all(self, w, M, dst):
        c, k = self.cfg, self.k
        for bi, (t0, n, cond) in enumerate(c.tblocks):
            pb = self.proj(w, M, t0, n, None)
            k.copy(dst[0:M, t0:t0 + n], pb[0:M, 0:n], eng=("act" if bi % 2 else "dve"))

    def shift_mix(self, src, dst, M, ci):
        c, k = self.cfg, self.k
        mu7 = self.mu7
        GW = c.GRID_W
        for (s0, sl, grid, cond) in c.seqs:
            S_ = src[0:M, s0:s0 + sl]
            D_ = dst[0:M, s0:s0 + sl]
            k.act(D_, S_, AF.Identity, scale=mu7[0:M, ci, 0:1])
            if not grid:
                k.stt(D_[:, 1:sl], S_[:, 0:sl - 1], mu7[0:M, ci, 1:2], D_[:, 1:sl], ALU.mult, ALU.add)
                k.stt(D_[:, 0:sl - 1], S_[:, 1:sl], mu7[0:M, ci, 2:3], D_[:, 0:sl - 1], ALU.mult, ALU.add)
            else:
                S3 = S_.re("p (r w) -> p r w", w=GW)
                D3 = D_.re("p (r w) -> p r w", w=GW)
                k.stt(D3[:, :, 1:GW], S3[:, :, 0:GW - 1], mu7[0:M, ci, 3:4], D3[:, :, 1:GW], ALU.mult, ALU.add)
                k.stt(D3[:, :, 0:GW - 1], S3[:, :, 1:GW], mu7[0:M, ci, 4:5], D3[:, :, 0:GW - 1], ALU.mult, ALU.add)
                k.stt(D_[:, GW:sl], S_[:, 0:sl - GW], mu7[0:M, ci, 5:6], D_[:, GW:sl], ALU.mult, ALU.add)
                k.stt(D_[:, 0:sl - GW], S_[:, GW:sl], mu7[0:M, ci, 6:7], D_[:, 0:sl - GW], ALU.mult, ALU.add)

    def tri_inverse_gen(self, MRa, MTa, Z, nu, MRb, MTb):
        k = self.k
        cur, oth = MRa, MRb
        ct, ot = MTa, MTb
        idb = self.tmask[0:64, 4:5, :].bc([64, nu, 64])
        k.tt(oth[0:64, :, 64:128], cur[0:64, :, 0:64], idb, ALU.add)
        for u0 in range(0, nu, 8):
            u1 = min(nu, u0 + 8)
            pm = k.bank()
            for u in range(u0, u1):
                k.mm(pm[0:64, (u - u0) * 64:(u - u0 + 1) * 64], ct[0:64, u, :], cur[0:64, u, 0:64])
            k.copy(oth[0:64, u0:u1, 0:64], pm[0:64, 0:(u1 - u0) * 64].re("p (a b) -> p a b", b=64), eng="act")
            yield
            pt = k.bank()
            for u in range(u0, u1):
                k.mm(pt[0:64, (u - u0) * 64:(u - u0 + 1) * 64], cur[0:64, u, 0:64], ct[0:64, u, :])
            k.copy(ot[0:64, u0:u1, :], pt[0:64, 0:(u1 - u0) * 64].re("p (a b) -> p a b", b=64), eng="act")
            yield
        cur, oth = oth, cur
        ct, ot = ot, ct
        for lev in range(1, 5):
            for u0 in range(0, nu, 4):
                u1 = min(nu, u0 + 4)
                pm = k.bank()
                for u in range(u0, u1):
                    k.mm(pm[0:64, (u - u0) * 128:(u - u0 + 1) * 128], ct[0:64, u, :], cur[0:64, u, :])
                p3 = pm[0:64, 0:(u1 - u0) * 128].re("p (a b) -> p a b", b=128)
                if lev < 4:
                    k.copy(oth[0:64, u0:u1, 0:64], p3[:, :, 0:64], eng="act")
                k.tt(oth[0:64, u0:u1, 64:128], cur[0:64, u0:u1, 64:128], p3[:, :, 64:128], ALU.add)
                yield
            for u0 in range(0, nu, 8):
                u1 = min(nu, u0 + 8)
                pt = k.bank()
                for u in range(u0, u1):
                    k.mm(pt[0:64, (u - u0) * 64:(u - u0 + 1) * 64], cur[0:64, u, 0:64], ct[0:64, u, :])
                k.copy(ot[0:64, u0:u1, :], pt[0:64, 0:(u1 - u0) * 64].re("p (a b) -> p a b", b=64), eng="act")
                yield
            cur, oth = oth, cur
            ct, ot = ot, ct
        for u0 in range(0, nu, 8):
            u1 = min(nu, u0 + 8)
            pz = k.bank()
            for u in range(u0, u1):
                k.mm(pz[0:64, (u - u0) * 64:(u - u0 + 1) * 64], ct[0:64, u, :], cur[0:64, u, 64:128])
            k.tt(Z[0:64, u0:u1, :], cur[0:64, u0:u1, 64:128], pz[0:64, 0:(u1 - u0) * 64].re("p (a b) -> p a b", b=64), ALU.add)
            yield

    def interleave(self, main, filler, ratio):
        for _ in main:
            for _r in range(ratio):
                if filler is not None:
                    try:
                        next(filler)
                    except StopIteration:
                        filler = None
        if filler is not None:
            for _ in filler:
                pass

    return dict(wload=wload, proj=proj, proj_all=proj_all, shift_mix=shift_mix, tri_inverse_gen=tri_inverse_gen, interleave=interleave)


for _n, _f in _ab_methods().items():
    setattr(MKB, _n, _f)


def _ab_layer_methods():
    def ab_layer(self, l):
        c, k, A = self.cfg, self.k, self.k.A
        i = l // 2
        DC = c.DC
        CAC = c.C_A // 128
        m0 = A.mark()
        self.hdep = [Dep() for _ in c.tblocks]
        self.hT = A.alloc([DC, c.NT], BF16, deps=self.hdep)
        for bi, (t0, n, cond) in enumerate(c.tblocks):
            self.norm_block(bi, self.AM, 0, lambda cc: T(self.hT.ap[:, cc, t0:t0 + n], [self.hdep[bi]]))
        allx = [self.xdep[cc][bi] for cc in range(DC) for bi in range(len(c.tblocks))]
        k.dma(self.d["xspill"], T(self.xT.ap.rearrange("p a b -> p (a b)"), allx))
        k.S.barrier()
        if DC * c.NT >= 24576:
            AX = Arena(None, None, 0, parent=A, base=self.xT_base, words=DC * c.NT)
        else:
            b0 = A.off
            A.alloc([24576])
            AX = Arena(None, None, 0, parent=A, base=b0, words=24576)
        self.AX = AX
        NMU = c.NMU
        self.mu7 = A.alloc([NMU, 7])
        om, _ = self.lay["mu"]
        mu = self.vecs[:, om + i * NMU:om + (i + 1) * NMU]
        k.ts(self.mu7[:, :, 0], mu, -1.0, ALU.mult, 1.0, ALU.add)
        for j in range(6):
            k.ts(self.mu7[:, :, 1 + j], mu, self.vec("pmask", j), ALU.mult)
        self.wring = [A.alloc([DC, 128], BF16) for _ in range(2)]
        self.wri = 0
        self.odep = [Dep() for _ in range(DC)]
        self.osb = [A.alloc([512], BF16) for _ in range(2)]
        self.osi = 0
        import os
        dbg = os.environ.get("MK_DBG", "")
        self.mark_phase("  rwkv")
        self.rwkv_part(i)
        self.mark_phase("  gdn")
        if "nogdn" in dbg:
            for cc in range(CAC, DC):
                for t0 in range(0, c.NT, 512):
                    n = min(512, c.NT - t0)
                    ob = self.o_tile()
                    k.memset(ob[:, 0:n], 0.0)
                    self.o_store(cc, t0, n, ob)
        else:
            self.gdn_part(i)
        self.mark_phase("  wout")
        A.release(m0)
        k.S.barrier()
        m0 = A.mark()
        k.dma(T(self.xT.ap.rearrange("p a b -> p (a b)"), allx), self.d["xspill"])
        self.oT = A.alloc([DC, c.NT], BF16, deps=self.odep)
        for kc in range(DC):
            k.dma(T(self.oT.ap[:, kc, :], [self.odep[kc]]), T(self.d["oscr"].ap[:, kc, :], [self.odep[kc]]))
        NWO = 2
        wo = [A.alloc([DC, 128], BF16) for _ in range(NWO)]
        wsrc = self.d["w_out"].ap[i].rearrange("(kc p) n -> p kc n", p=128)
        for dc in range(DC):
            w = wo[dc % NWO]
            k.dma(w, T(wsrc[:, :, dc * 128:(dc + 1) * 128], []), eng="pool")
            for bi, (t0, n, cond) in enumerate(c.tblocks):
                po = k.bank()
                for kc in range(DC):
                    k.mm(po[:, 0:n], w[:, kc, :], T(self.oT.ap[:, kc, t0:t0 + n], [self.odep[kc]]), start=(kc == 0), stop=(kc == DC - 1))
                x = self.xt(dc, bi)
                k.stt(x, po[:, 0:n], self.mod(2, dc, cond), x, ALU.mult, ALU.add)
        A.release(m0)
        k.S.barrier()

    def rwkv_part(self, i):
        c, k, A, AX = self.cfg, self.k, self.k.A, self.AX
        DC = c.DC
        CAC = c.C_A // 128
        NT = c.NT
        mA = A.mark()
        praw = A.alloc([NT])
        pmix = A.alloc([NT])
        wlo = A.alloc([NT], BF16)
        alo = A.alloc([NT], BF16)
        glo = A.alloc([NT], BF16)
        glo1 = A.alloc([NT], BF16)
        base = 3 * c.C_A
        for (col, M, ci, dst, fn) in [(base, 128, 3 * CAC, wlo, AF.Tanh), (base + 128, 128, 3 * CAC + 1, alo, AF.Identity),
                                      (base + 256, 128, 3 * CAC + 2, glo, AF.Sigmoid), (base + 384, 32, 3 * CAC + 3, glo1, AF.Sigmoid)]:
            w = self.wload(i, col, M)
            self.proj_all(w, M, praw)
            self.shift_mix(praw, pmix, M, ci)
            k.act(dst[0:M], pmix[0:M], fn)
        w2s = A.alloc([128], BF16)
        a2s = A.alloc([128], BF16)
        g2s = A.alloc([128], BF16)
        g2s1 = A.alloc([128], BF16)
        lnb2 = A.alloc([CAC])
        ol, _ = self.lay["lnb"]
        k.ts(lnb2, self.vecs[:, ol + i * CAC:ol + (i + 1) * CAC], 2.0, ALU.mult)
        CG = 4
        import os
        stopat = int(os.environ.get("MK_STOP", "99"))
        if stopat <= 1:
            A.release(mA)
            return
        for hp in range(CAC):
            mX = AX.mark()
            mA2 = A.mark()
            hs_ = slice(hp * 128, (hp + 1) * 128)
            k.dma(w2s, T(self.d["w2"].ap[i][:, hs_], []), eng="pool")
            k.dma(a2s, T(self.d["a2"].ap[i][:, hs_], []), eng="pool")
            k.dma(g2s, T(self.d["g2"].ap[i, 0:128, hs_], []), eng="pool")
            k.dma(g2s1[0:32], T(self.d["g2"].ap[i, 128:160, hs_], []), eng="pool")
            hs = slice(0, 128)
            rkv = []
            for j in range(3):
                w = self.wload(i, j * c.C_A + hp * 128, 128)
                self.proj_all(w, 128, praw)
                dst = AX.alloc([NT])
                self.shift_mix(praw, dst, 128, j * CAC + hp)
                rkv.append(dst)
            rT, kT, vT = rkv
            vb = AX.alloc([NT], BF16)
            k.copy(vb, vT, eng="pool")
            kk = AX.alloc([NT])
            okk, _ = self.lay["kk"]
            k.ts(kk, kT, self.vecs[:, okk + i * CAC + hp:okk + i * CAC + hp + 1], ALU.mult)
            for bi, (t0, n, cond) in enumerate(c.tblocks):
                k.act(praw[:, t0:t0 + n], kk[:, t0:t0 + n], AF.Square)
                pb = k.bank()
                k.mm(pb[:, 0:n], self.blockones, praw[:, t0:t0 + n])
                k.rsqrt(pmix[:, t0:t0 + n], pb[:, 0:n], 1e-12)
            k.tt(kk, kk, pmix, ALU.mult)
            bonus = AX.alloc([NT])
            ow0, _ = self.lay["w0"]
            oa0, _ = self.lay["a0"]
            oka, _ = self.lay["ka"]
            ork, _ = self.lay["rk"]
            oln, _ = self.lay["lnw"]
            ka_ = self.vecs[:, oka + i * CAC + hp:oka + i * CAC + hp + 1]
            rk_ = self.vecs[:, ork + i * CAC + hp:ork + i * CAC + hp + 1]
            lnw_ = self.vecs[:, oln + i * CAC + hp:oln + i * CAC + hp + 1]
            if stopat <= 2:
                break
            for si, (s0, sl, grid, cond) in enumerate(c.seqs):
                mS = AX.mark()
                mS2 = A.mark()
                nch = sl // 64
                ss = slice(s0, s0 + sl)
                Vtm = AX.alloc([nch, 128], BF16)
                self.to_tm(vb[:, ss], Vtm, nch)
                yn = [AX.alloc([nch, 128], BF16) for _ in range(2)]
                bsum_ps = []
                for d in range(2):
                    mD = AX.mark()
                    mD2 = A.mark()
                    at = AX.alloc([sl], BF16)
                    rt = AX.alloc([sl], BF16)
                    bt = AX.alloc([sl], BF16)
                    kt = AX.alloc([sl], BF16)
                    Pinc = AX.alloc([sl])
                    Btm = AX.alloc([nch, 128], BF16)
                    Ktm = AX.alloc([nch, 128], BF16)
                    mT = AX.mark()
                    lw = AX.alloc([sl])
                    ic = AX.alloc([sl])
                    for o in range(0, sl, 512):
                        n = min(512, sl - o)
                        pb = k.bank()
                        k.mm(pb[:, 0:n], w2s[64 * d:64 * d + 64, hs], wlo[64 * d:64 * d + 64, s0 + o:s0 + o + n])
                        k.act(lw[:, o:o + n], pb[:, 0:n], AF.Sigmoid, bias=self.vecs[:, ow0 + (i * 2 + d) * CAC + hp:ow0 + (i * 2 + d) * CAC + hp + 1])
                        pb2 = k.bank()
                        k.mm(pb2[:, 0:n], a2s[64 * d:64 * d + 64, hs], alo[64 * d:64 * d + 64, s0 + o:s0 + o + n])
                        k.act(ic[:, o:o + n], pb2[:, 0:n], AF.Sigmoid, bias=self.vecs[:, oa0 + (i * 2 + d) * CAC + hp:oa0 + (i * 2 + d) * CAC + hp + 1])
                    k.ts(lw, lw, -0.6065306597126334, ALU.mult)
                    cumI = AX.alloc([sl])
                    self.chunk_cumsum(lw, cumI, sl, d, AX)
                    cumE = AX.alloc([sl])
                    k.tt(cumE, cumI, lw, ALU.subtract)
                    k.act(Pinc, cumI, AF.Exp)
                    k.act(cumE, cumE, AF.Exp)
                    k.act(cumI, cumI, AF.Exp, scale=-1.0)
                    Pexc, Pinv = cumE, cumI
                    kd = AX.alloc([sl])
                    k.ts(kd, ic, 1.0, ALU.subtract, ka_, ALU.mult)
                    k.stt(kd, kd, 1.0, kT[:, ss], ALU.add, ALU.mult)
                    k.stt(at, kk[:, ss], -1.0, Pexc, ALU.mult, ALU.mult)
                    k.tt(rt, rT[:, ss], Pinc, ALU.mult)
                    k.tt(lw, kk[:, ss], ic, ALU.mult)
                    k.tt(bt, lw, Pinv, ALU.mult)
                    k.tt(kt, kd, Pinv, ALU.mult)
                    k.stt(kd, rT[:, ss], rk_, kd, ALU.mult, ALU.mult)
                    rkr = kd
                    for o in range(0, sl, 512):
                        n = min(512, sl - o)
                        pb = k.bank()
                        k.mm(pb[:, 0:n], self.blockones, rkr[:, o:o + n])
                        if d == 0:
                            k.tt(bonus[:, s0 + o:s0 + o + n], pb[:, 0:n], vT[:, s0 + o:s0 + o + n], ALU.mult)
                        else:
                            k.tt(ic[:, o:o + n], pb[:, 0:n], vT[:, s0 + o:s0 + o + n], ALU.mult)
                            k.tt(bonus[:, s0 + o:s0 + o + n], bonus[:, s0 + o:s0 + o + n], ic[:, o:o + n], ALU.add)
                    if stopat <= 3:
                        break
                    self.to_tm(bt, Btm, nch)
                    self.to_tm(kt, Ktm, nch)
                    AX.release(mT)
                    Sf = A.alloc([64])
                    Sb = A.alloc([64], BF16)
                    self.rwkv_init_state(i, d, hp, grid, Sf, Sb)
                    order = list(range(nch)) if d == 0 else list(range(nch - 1, -1, -1))
                    if stopat <= 4:
                        break
                    self.rwkv_seqdir(d, order, at, rt, bt, kt, Pinc, Vtm, Btm, Ktm, Sf, Sb, yn[d])
                    if not grid:
                        self.rwkv_store_state(i, d, hp, si, Sf)
                    A.release(mD2)
                    AX.release(mD)
                if stopat <= 6:
                    break
                for n0 in range(0, nch, 8):
                    n1 = min(nch, n0 + 8)
                    pb = k.bank()
                    for n_ in range(n0, n1):
                        cs = slice((n_ - n0) * 64, (n_ - n0 + 1) * 64)
                        k.mm(pb[:, cs], yn[0][0:64, n_, :], self.identb[0:64, 0:64], start=True, stop=False)
                        k.mm(pb[:, cs], yn[1][0:64, n_, :], self.identb[0:64, 0:64], start=False, stop=True)
                    ts_ = slice(s0 + n0 * 64, s0 + n1 * 64)
                    nn = (n1 - n0) * 64
                    k.stt(praw[:, 0:nn], pb[:, 0:nn], lnw_, bonus[:, ts_], ALU.mult, ALU.add)
                    pg = k.bank()
                    k.mm(pg[:, 0:nn], g2s[:, hs], glo[:, ts_], start=True, stop=False)
                    k.mm(pg[:, 0:nn], g2s1[0:32, hs], glo1[0:32, ts_], start=False, stop=True)
                    ob = self.o_tile()
                    k.stt(ob[:, 0:nn], praw[:, 0:nn], lnb2[:, hp:hp + 1], pg[:, 0:nn], ALU.add, ALU.mult)
                    self.o_store(hp, s0 + n0 * 64, nn, ob)
                A.release(mS2)
                AX.release(mS)
            A.release(mA2)
            AX.release(mX)
        A.release(mA)

    def dump(self, name, t, parts, free):
        import os
        if "dump" not in os.environ.get("MK_DBG", ""):
            return
        dt_ = t.ap.dtype
        dr = T(self.nc.dram_tensor(name, [parts, free], dt_, kind="Internal").ap(), [Dep()])
        self.k.dma(dr, t)

    def o_tile(self):
        t = self.osb[self.osi % 2]
        self.osi += 1
        return t

    def o_store(self, cc, t0, n, ob):
        self.k.dma(T(self.d["oscr"].ap[:, cc, t0:t0 + n], [self.odep[cc]]), ob[:, 0:n])

    def to_tm(self, src, dst, nch):
        k = self.k
        for n0 in range(0, nch, 8):
            n1 = min(nch, n0 + 8)
            pb = k.bank()
            pbb = T(pb.ap.bitcast(BF16), pb.deps, True)
            for n_ in range(n0, n1):
                k.tr(pbb[0:64, (n_ - n0) * 128:(n_ - n0 + 1) * 128], src[:, n_ * 64:(n_ + 1) * 64], self.identb)
            k.copy(dst[0:64, n0:n1, :], pbb[0:64, 0:(n1 - n0) * 128].re("p (a b) -> p a b", b=128), eng="act")

    def chunk_cumsum(self, src, dst, sl, d, AR):
        k = self.k
        P = src.ap.shape[0]
        m = AR.mark()
        mask = AR.alloc([sl])
        k.memset(mask[0:P], 1.0, eng="pool")
        k.memset(mask[0:P].re("p (a b) -> p a b", b=64)[:, :, 0:1], 0.0, eng="pool")
        k.scan(dst, mask[0:P], src, 0.0, ALU.mult, ALU.add)
        if d == 1:
            nch = sl // 64
            d3 = dst.re("p (a b) -> p a b", b=64)
            tot = d3[:, :, 63:64].bc([P, nch, 64])
            k.tt(mask[0:P].re("p (a b) -> p a b", b=64), tot, d3, ALU.subtract)
            k.tt(dst, mask[0:P], src, ALU.add)
        AR.release(m)

    def rwkv_init_state(self, i, d, hp, grid, Sf, Sb):
        c, k, A = self.cfg, self.k, self.k.A
        import os
        if not grid or "noinit" in os.environ.get("MK_DBG", ""):
            k.memset(Sf, 0.0)
            k.memset(Sb, 0.0)
            return
        m = A.mark()
        st2 = A.alloc([2, 64])
        for e in range(2):
            r0 = ((i * 2 + d) * c.H_A + 2 * hp + e) * 64
            if "xsrc" in os.environ.get("MK_DBG", ""):
                k.dma(st2[0:64, e, :], T(self.d["xs"].ap[0:64, e * 64:(e + 1) * 64], []))
            else:
                k.dma(st2[0:64, e, :], T(self.d["st_r_in"].ap[r0:r0 + 64, :], []))
        pb = k.bank()
        k.mm(pb[:, 0:64], st2[0:64].re("p a b -> p (a b)"), self.ident[0:64, 0:64])
        k.copy(Sf, pb[:, 0:64])
        k.copy(Sb, pb[:, 0:64], eng="act")
        A.release(m)

    def rwkv_store_state(self, i, d, hp, si, Sf):
        c, k, A = self.cfg, self.k, self.k.A
        m = A.mark()
        pb = k.bank()
        k.mm(pb[0:64, 0:128], Sf, self.ident)
        o = A.alloc([128])
        k.copy(o[0:64], pb[0:64, 0:128])
        for e in range(2):
            r0 = ((((si - 1) * c.N_AB + i) * 2 + d) * c.H_A + 2 * hp + e) * 64
            k.dma(T(self.d["st_r"].ap[r0:r0 + 64, :], self.d["st_r"].deps), o[0:64, e * 64:(e + 1) * 64], is_output=True)
        A.release(m)

    def rwkv_seqdir(self, d, order, at, rt, bt, kt, Pinc, Vtm, Btm, Ktm, Sf, Sb, yn):
        c, k, A = self.cfg, self.k, self.k.A
        m = A.mark()
        CG = 2
        groups = [order[g0:g0 + CG] for g0 in range(0, len(order), CG)]
        nu = 2 * CG
        mS, mI = (0, 1) if d == 0 else (2, 3)
        mST = 2 if d == 0 else 0
        sets = [dict(MR=A.alloc([nu, 128]), MTn=A.alloc([nu, 64]), Z=A.alloc([nu, 64])) for _ in range(2)]
        Mb = A.alloc([nu, 128])
        MTb = A.alloc([nu, 64])
        Aak = A.alloc([nu, 64], BF16)
        Arb = A.alloc([nu, 64], BF16)
        Ark = A.alloc([nu, 64], BF16)
        W1s = A.alloc([2, 64])
        Ub = A.alloc([2, 64], BF16)
        st6 = A.alloc([2, 6])
        mv = A.alloc([2, 2])
        rstd = A.alloc([2])
        SP = A.alloc([64])

        def amat(dst, L, R, mask, grp):
            ng = len(grp)
            pb = k.bank()
            for u in range(2 * ng):
                e, gi = divmod(u, ng)
                cs = slice(grp[gi] * 64, grp[gi] * 64 + 64)
                ps = slice(64 * e, 64 * e + 64)
                k.mm(pb[0:64, u * 64:(u + 1) * 64], L[ps, cs], R[ps, cs])
            k.tt(dst[0:64, 0:2 * ng, :], pb[0:64, 0:2 * ng * 64].re("p (a b) -> p a b", b=64),
                 self.tmask[0:64, mask:mask + 1, :].bc([64, 2 * ng, 64]), ALU.mult)

        def prep(grp, S_):
            amat(S_["MR"][:, :, 0:64], bt, at, mS, grp)
            yield
            amat(S_["MTn"], at, bt, mST, grp)
            yield
            ngu = 2 * len(grp)
            yield from self.tri_inverse_gen(S_["MR"][:, 0:ngu, :], S_["MTn"][:, 0:ngu, :], S_["Z"][:, 0:ngu, :], ngu, Mb[:, 0:ngu, :], MTb[:, 0:ngu, :])

        def chunks(grp, S_):
            ng = len(grp)
            Z = S_["Z"]
            amat(Aak, kt, at, mS, grp)
            amat(Arb, bt, rt, mI, grp)
            amat(Ark, kt, rt, mI, grp)
            for gi, n_ in enumerate(grp):
                cs = slice(n_ * 64, n_ * 64 + 64)
                pc = Pinc[:, n_ * 64 + 63:n_ * 64 + 64] if d == 0 else Pinc[:, n_ * 64:n_ * 64 + 1]
                k.act(SP, Sf, AF.Identity, scale=pc)
                pw = k.bank()
                for e in range(2):
                    ps = slice(64 * e, 64 * e + 64)
                    u = e * ng + gi
                    k.mm(pw[0:64, e * 64:(e + 1) * 64], Aak[0:64, u, :], Vtm[0:64, n_, e * 64:(e + 1) * 64], start=True, stop=False)
                    k.mm(pw[0:64, e * 64:(e + 1) * 64], at[ps, cs], Sb[ps, :], start=False, stop=True)
                k.copy(W1s[0:64].re("p a b -> p (a b)"), pw[0:64, 0:128])
                yield
                pu = k.bank()
                for e in range(2):
                    u = e * ng + gi
                    k.mm(pu[0:64, e * 64:(e + 1) * 64], Z[0:64, u, :], W1s[0:64, e, :])
                k.copy(Ub[0:64].re("p a b -> p (a b)"), pu[0:64, 0:128], eng="act")
                yield
                py = k.bank()
                pd = k.bank()
                for e in range(2):
                    ps = slice(64 * e, 64 * e + 64)
                    u = e * ng + gi
                    yo = py[0:64, e * 64:(e + 1) * 64]
                    k.mm(yo, Arb[0:64, u, :], Ub[0:64, e, :], start=True, stop=False)
                    k.mm(yo, Ark[0:64, u, :], Vtm[0:64, n_, e * 64:(e + 1) * 64], start=False, stop=False)
                    k.mm(yo, rt[ps, cs], Sb[ps, :], start=False, stop=True)
                    do = pd[ps, 0:64]
                    k.mm(do, Btm[0:64, n_, ps], Ub[0:64, e, :], start=True, stop=False)
                    k.mm(do, Ktm[0:64, n_, ps], Vtm[0:64, n_, e * 64:(e + 1) * 64], start=False, stop=True)
                k.stt(Sf, pd[:, 0:64], pc, SP, ALU.mult, ALU.add)
                k.copy(Sb, Sf, eng="act")
                for e in range(2):
                    k.bnstats(st6[0:64, e, :], py[0:64, e * 64:(e + 1) * 64])
                    k.bnaggr(mv[0:64, e, :], st6[0:64, e, :])
                k.rsqrt(rstd[0:64], mv[0:64, :, 1], 64e-5)
                for e in range(2):
                    k.ts(yn[0:64, n_, e * 64:(e + 1) * 64], py[0:64, e * 64:(e + 1) * 64], mv[0:64, e, 0:1], ALU.subtract, rstd[0:64, e:e + 1], ALU.mult)
                yield
        for _ in prep(groups[0], sets[0]):
            pass
        for g, grp in enumerate(groups):
            filler = prep(groups[g + 1], sets[(g + 1) % 2]) if g + 1 < len(groups) else None
            self.interleave(chunks(grp, sets[g % 2]), filler, 3)
        A.release(m)

    return dict(ab_layer=ab_layer, rwkv_part=rwkv_part, to_tm=to_tm, dump=dump, o_tile=o_tile, o_store=o_store, chunk_cumsum=chunk_cumsum, rwkv_init_state=rwkv_init_state,
                rwkv_store_state=rwkv_store_state, rwkv_seqdir=rwkv_seqdir)


for _n, _f in _ab_layer_methods().items():
    setattr(MKB, _n, _f)


def _gdn_methods():
    def gdn_part(self, i):
        c, k, A, AX = self.cfg, self.k, self.k.A, self.AX
        DC, NT, H_B = c.DC, c.NT, c.H_B
        CAC = c.C_A // 128
        CBC = c.C_B // 128
        PB0 = c.C_PA
        M2 = 2 * H_B
        mA = A.mark()
        mX0 = AX.mark()
        praw = A.alloc([NT])
        pmix = A.alloc([NT])
        NCH = NT // 64
        colsT = A.alloc([NCH, 96])
        mR = A.mark()
        rows = A.alloc([NT])
        graw = A.alloc([NT])
        k.memset(rows, 0.0, eng="pool")
        oal, _ = self.lay["alog"]
        odt, _ = self.lay["dtb"]
        nea = A.alloc([1])
        k.act(nea[0:M2], self.vecs[0:M2, oal + i:oal + i + 1], AF.Exp)
        k.ts(nea[0:M2], nea[0:M2], -1.0, ALU.mult)
        wbeta = self.wload(i, PB0 + 4 * c.C_B, M2)
        for bi, (t0, n, cond) in enumerate(c.tblocks):
            pb = self.proj(wbeta, M2, t0, n, None)
            k.act(rows[64:64 + M2, t0:t0 + n], pb[0:M2, 0:n], AF.Sigmoid)
        walpha = self.wload(i, PB0 + 4 * c.C_B + M2, M2)
        for bi, (t0, n, cond) in enumerate(c.tblocks):
            pb = self.proj(walpha, M2, t0, n, None)
            x = praw[0:M2, t0:t0 + n]
            ax = pmix[0:M2, t0:t0 + n]
            k.act(x, pb[0:M2, 0:n], AF.Identity, bias=self.vecs[0:M2, odt + i:odt + i + 1])
            k.ts(ax, x, -1.0, ALU.mult)
            k.tt(ax, ax, x, ALU.max)
            k.act(ax, ax, AF.Exp, scale=-1.0)
            k.act(ax, ax, AF.Ln, bias=1.0)
            k.stt(x, x, 0.0, ax, ALU.max, ALU.add)
            k.ts(graw[0:M2, t0:t0 + n], x, nea[0:M2], ALU.mult)
        for (s0, sl, grid, cond) in c.seqs:
            ss = slice(s0, s0 + sl)
            self.chunk_cumsum(graw[0:M2, ss], rows[0:M2, ss], sl, 0, A)
            self.chunk_cumsum(graw[0:M2, ss], rows[32:32 + M2, ss], sl, 1, A)
        k.dma(self.d["rowsc"], rows[0:96, :])
        for n0 in range(0, NCH, 5):
            n1 = min(NCH, n0 + 5)
            pb = k.bank()
            for n_ in range(n0, n1):
                k.tr(pb[0:64, (n_ - n0) * 96:(n_ - n0 + 1) * 96], rows[0:96, n_ * 64:(n_ + 1) * 64], self.ident[0:96, 0:96])
            k.copy(colsT[0:64, n0:n1, :], pb[0:64, 0:(n1 - n0) * 96].re("p (a b) -> p a b", b=96))
        A.release(mR)
        ocw, _ = self.lay["convw"]
        ogn, _ = self.lay["gnw"]
        for h in range(H_B):
            mX = AX.mark()
            mA2 = A.mark()
            qb = AX.alloc([NT], BF16)
            kb = AX.alloc([NT], BF16)
            vb = AX.alloc([NT], BF16)
            zs_ = AX.alloc([NT])
            mQ = AX.mark()
            fm = []
            for j in range(4):
                w = self.wload(i, PB0 + j * c.C_B + h * 128, 128)
                self.proj_all(w, 128, praw)
                if j < 3:
                    cw = lambda tap: self.vecs[:, ocw + (i * 3 + tap) * 3 * CBC + j * CBC + h:ocw + (i * 3 + tap) * 3 * CBC + j * CBC + h + 1]
                    for (s0, sl, grid, cond) in c.seqs:
                        S_ = praw[:, s0:s0 + sl]
                        D_ = pmix[:, s0:s0 + sl]
                        k.act(D_, S_, AF.Identity, scale=cw(1))
                        k.stt(D_[:, 1:sl], S_[:, 0:sl - 1], cw(0), D_[:, 1:sl], ALU.mult, ALU.add)
                        k.stt(D_[:, 0:sl - 1], S_[:, 1:sl], cw(2), D_[:, 0:sl - 1], ALU.mult, ALU.add)
                    dst = AX.alloc([NT])
                    k.act(dst, pmix, AF.Silu)
                else:
                    dst = zs_
                    k.act(dst, praw, AF.Silu)
                fm.append(dst)
            qf, kf, vf, zs = fm
            k.copy(vb, vf, eng="pool")
            for (src, dstb, sc) in ((qf, qb, 128.0 ** -0.5), (kf, kb, 1.0)):
                for bi, (t0, n, cond) in enumerate(c.tblocks):
                    k.act(praw[:, t0:t0 + n], src[:, t0:t0 + n], AF.Square)
                    pb = k.bank()
                    k.mm(pb[:, 0:n], self.onesf, praw[:, t0:t0 + n])
                    k.rsqrt(pmix[:, t0:t0 + n], pb[:, 0:n], 1e-6)
                    k.stt(dstb[:, t0:t0 + n], src[:, t0:t0 + n], sc, pmix[:, t0:t0 + n], ALU.mult, ALU.mult)
            AX.release(mQ)
            for si, (s0, sl, grid, cond) in enumerate(c.seqs):
                mS = AX.mark()
                mS2 = A.mark()
                nch = sl // 64
                co = s0 // 64
                ss = slice(s0, s0 + sl)
                Ktm = AX.alloc([nch, 128], BF16)
                Vtm = AX.alloc([nch, 128], BF16)
                self.to_tm(kb[:, ss], Ktm, nch)
                self.to_tm(vb[:, ss], Vtm, nch)
                QK = AX.alloc([nch, 64])
                KK = AX.alloc([nch, 64])
                for (dst, R_) in ((QK, qb), (KK, kb)):
                    for n0 in range(0, nch, 8):
                        n1 = min(nch, n0 + 8)
                        pb = k.bank()
                        for n_ in range(n0, n1):
                            cs = slice(s0 + n_ * 64, s0 + n_ * 64 + 64)
                            k.mm(pb[0:64, (n_ - n0) * 64:(n_ - n0 + 1) * 64], kb[:, cs], R_[:, cs])
                        k.copy(dst[0:64, n0:n1, :], pb[0:64, 0:(n1 - n0) * 64].re("p (a b) -> p a b", b=64), eng="act")
                osum = AX.alloc([nch, 128])
                for d in range(2):
                    mD = AX.mark()
                    mD2 = A.mark()
                    r = d * H_B + h
                    grow = (0 if d == 0 else 32) + r
                    Gbc = AX.alloc([sl])
                    Bbc = AX.alloc([sl])
                    k.dma(Gbc, T(self.d["rowsc"].ap[grow:grow + 1, ss].partition_broadcast(128), self.d["rowsc"].deps))
                    k.dma(Bbc, T(self.d["rowsc"].ap[64 + r:65 + r, ss].partition_broadcast(128), self.d["rowsc"].deps))
                    Gcol = colsT[0:64, co:co + nch, grow]
                    Bcol = colsT[0:64, co:co + nch, 64 + r]
                    eG = AX.alloc([sl])
                    k.act(eG, Gbc, AF.Exp)
                    kbg = AX.alloc([sl], BF16)
                    qg = AX.alloc([sl], BF16)
                    tmpf = AX.alloc([sl])
                    k.tt(tmpf, Bbc, eG, ALU.mult)
                    k.tt(kbg, kb[:, ss], tmpf, ALU.mult)
                    k.tt(qg, qb[:, ss], eG, ALU.mult)
                    G3 = Gbc[0:64].re("p (n c) -> p n c", c=64)
                    B3 = Bbc[0:64].re("p (n c) -> p n c", c=64)
                    Gc3 = Gcol.v(lambda a: a.unsqueeze(2)).bc([64, nch, 64])
                    Bc3 = Bcol.v(lambda a: a.unsqueeze(2)).bc([64, nch, 64])
                    mInc, mStr, mStrT = (1, 0, 2) if d == 0 else (3, 2, 0)
                    tm_ = lambda idx: self.tmask[0:64, idx:idx + 1, :].bc([64, nch, 64])
                    DmT = AX.alloc([nch, 64])
                    k.tt(DmT[0:64], G3, Gc3, ALU.subtract)
                    k.ts(DmT[0:64], DmT[0:64], 0.0, ALU.min)
                    k.act(DmT[0:64], DmT[0:64], AF.Exp)
                    attnT = AX.alloc([nch, 64], BF16)
                    MRn = AX.alloc([nch, 128])
                    Mn = MRn[:, :, 0:64]
                    MTn = AX.alloc([nch, 64])
                    Z = AX.alloc([nch, 64])
                    t3 = tmpf[0:64].re("p (n c) -> p n c", c=64)
                    k.tt(t3, DmT[0:64], tm_(mInc), ALU.mult)
                    k.tt(attnT[0:64], QK[0:64], t3, ALU.mult)
                    k.tt(t3, DmT[0:64], tm_(mStr), ALU.mult)
                    k.tt(t3, t3, B3, ALU.mult)
                    k.stt(Mn[0:64], KK[0:64], -1.0, t3, ALU.mult, ALU.mult)
                    k.tt(t3, Gc3, G3, ALU.subtract)
                    k.ts(t3, t3, 0.0, ALU.min)
                    k.act(t3, t3, AF.Exp)
                    k.tt(t3, t3, tm_(mStrT), ALU.mult)
                    k.tt(t3, t3, Bc3, ALU.mult)
                    k.stt(MTn[0:64], KK[0:64], -1.0, t3, ALU.mult, ALU.mult)
                    egl = A.alloc([nch])
                    Glast = G3[:, :, 63] if d == 0 else G3[:, :, 0]
                    k.tt(egl[0:64], Glast, Gcol, ALU.subtract)
                    k.act(egl[0:64], egl[0:64], AF.Exp)
                    kdtm = AX.alloc([nch, 128], BF16)
                    vbtm = AX.alloc([nch, 128])
                    k.tt(kdtm[0:64], Ktm[0:64], egl[0:64].v(lambda a: a.unsqueeze(2)).bc([64, nch, 128]), ALU.mult)
                    k.tt(vbtm[0:64], Vtm[0:64], Bcol.v(lambda a: a.unsqueeze(2)).bc([64, nch, 128]), ALU.mult)
                    Sf = A.alloc([128])
                    Sb = A.alloc([128], BF16)
                    if grid:
                        r0 = ((i * 2 + d) * H_B + h) * 128
                        k.dma(Sf, T(self.d["st_d_in"].ap[r0:r0 + 128, :], []))
                    else:
                        k.memset(Sf, 0.0)
                    k.copy(Sb, Sf, eng="act")
                    W1s = A.alloc([128])
                    vnb = A.alloc([128], BF16)
                    order = list(range(nch)) if d == 0 else list(range(nch - 1, -1, -1))
                    CGD = 4
                    ggrps = [order[g0:g0 + CGD] for g0 in range(0, nch, CGD)]
                    Mbt = A.alloc([CGD, 128])
                    MTbt = A.alloc([CGD, 64])

                    def gprep(grp):
                        u0, u1 = min(grp), max(grp) + 1
                        yield from self.tri_inverse_gen(MRn[:, u0:u1, :], MTn[:, u0:u1, :], Z[:, u0:u1, :], u1 - u0, Mbt[:, 0:u1 - u0, :], MTbt[:, 0:u1 - u0, :])

                    def gchunks(grp):
                        for n_ in grp:
                            cs = slice(n_ * 64, n_ * 64 + 64)
                            pw = k.bank()
                            k.mm(pw[0:64, 0:128], kbg[:, cs], Sb)
                            k.tt(W1s[0:64], vbtm[0:64, n_, :], pw[0:64, 0:128], ALU.subtract)
                            yield
                            pv = k.bank()
                            k.mm(pv[0:64, 0:128], Z[0:64, n_, :], W1s[0:64])
                            k.copy(vnb[0:64], pv[0:64, 0:128], eng="act")
                            yield
                            po = k.bank()
                            k.mm(po[0:64, 0:128], qg[:, cs], Sb, start=True, stop=False)
                            k.mm(po[0:64, 0:128], attnT[0:64, n_, :], vnb[0:64], start=False, stop=True)
                            pd = k.bank()
                            k.mm(pd[:, 0:128], kdtm[0:64, n_, :], vnb[0:64])
                            gc = eG[:, n_ * 64 + 63:n_ * 64 + 64] if d == 0 else eG[:, n_ * 64:n_ * 64 + 1]
                            k.stt(Sf, Sf, gc, pd[:, 0:128], ALU.mult, ALU.add)
                            k.copy(Sb, Sf, eng="act")
                            if d == 0:
                                k.copy(osum[0:64, n_, :], po[0:64, 0:128], eng="pool" if False else "dve")
                            else:
                                k.tt(osum[0:64, n_, :], osum[0:64, n_, :], po[0:64, 0:128], ALU.add)
                            yield
                    for _ in gprep(ggrps[0]):
                        pass
                    for g, grp in enumerate(ggrps):
                        filler = gprep(ggrps[g + 1]) if g + 1 < len(ggrps) else None
                        self.interleave(gchunks(grp), filler, 2)
                    if not grid:
                        r0 = (((si - 1) * c.N_AB + i) * 2 + d) * H_B * 128 + h * 128
                        k.dma(T(self.d["st_d"].ap[r0:r0 + 128, :], self.d["st_d"].deps), Sf, is_output=True)
                    A.release(mD2)
                    AX.release(mD)
                osq = AX.alloc([nch, 128])
                ssq = A.alloc([nch])
                for n_ in range(nch):
                    k.act_acc(osq[0:64, n_, :], osum[0:64, n_, :], AF.Square, ssq[0:64, n_:n_ + 1])
                k.rsqrt(ssq[0:64], ssq[0:64], 1e-6, scale=1.0 / 128)
                k.tt(osq[0:64], osum[0:64], ssq[0:64].v(lambda a: a.unsqueeze(2)).bc([64, nch, 128]), ALU.mult)
                for n0 in range(0, nch, 8):
                    n1 = min(nch, n0 + 8)
                    pb = k.bank()
                    for n_ in range(n0, n1):
                        k.mm(pb[:, (n_ - n0) * 64:(n_ - n0 + 1) * 64], osq[0:64, n_, :], self.ident[0:64, 0:64])
                    nn = (n1 - n0) * 64
                    ts_ = slice(s0 + n0 * 64, s0 + n1 * 64)
                    ob = self.o_tile()
                    k.stt(ob[:, 0:nn], pb[:, 0:nn], self.vecs[:, ogn + i:ogn + i + 1], zs[:, ts_], ALU.mult, ALU.mult)
                    self.o_store(CAC + h, s0 + n0 * 64, nn, ob)
                A.release(mS2)
                AX.release(mS)
            A.release(mA2)
            AX.release(mX)
        A.release(mA)
        AX.release(mX0)

    return dict(gdn_part=gdn_part)


AX_X = AX.X
for _n, _f in _gdn_methods().items():
    setattr(MKB, _n, _f)
```
